# Optimizing a Trainium2 kernel written in Bass

```python
import math
import jax
import jax.numpy as jnp
from jax import lax
import numpy as np

D_MODEL = 1024
BATCH = 2
SEQ = 8192
DEPTH = 2
DEC_BATCH = 128
DEC_SEQ = 4
PAST_LEN = 2048
PAGE_SIZE = 128

SB_HEAD_DIM = 64
SB_HEADS = (D_MODEL // 2) // SB_HEAD_DIM
SB_WIDTH = SB_HEADS * SB_HEAD_DIM
SB_QBLOCK = 128
SB_BIAS_MIN = 4.0
SB_BIAS_MAX = 8.0
POOL_WINDOWS = (2, 4, 8, 16)
POOL_GROUPS = 4
POOL_WIDTH = D_MODEL // 2
POOL_GROUP_DIM = POOL_WIDTH // POOL_GROUPS
POOL_STATE = max(POOL_WINDOWS) - 1
GLA_HEADS = 4
GLA_KEY_WIDTH = D_MODEL // 2
GLA_VAL_WIDTH = D_MODEL
GLA_DK = GLA_KEY_WIDTH // GLA_HEADS
GLA_DV = GLA_VAL_WIDTH // GLA_HEADS
GLA_RANK = 16
GLA_TAU = 16.0
GLA_CHUNK = 64
N_BRANCH = 3
EPS = 1e-6
N_IN = 4 * SB_WIDTH + 2 * POOL_WIDTH + 2 * GLA_KEY_WIDTH + 2 * GLA_VAL_WIDTH + GLA_RANK + N_BRANCH * D_MODEL

kernel_name = "hybrid_stickbreak_pool_gla_step"


def _split_points():
    sizes = (SB_WIDTH, SB_WIDTH, SB_WIDTH, SB_WIDTH,
             POOL_WIDTH, POOL_WIDTH,
             GLA_KEY_WIDTH, GLA_KEY_WIDTH, GLA_VAL_WIDTH,
             GLA_VAL_WIDTH, GLA_RANK,
             N_BRANCH * D_MODEL)
    pts, acc = [], 0
    for s in sizes[:-1]:
        acc += s
        pts.append(acc)
    return pts


def rms_norm(x, g):
    xf = x.astype(jnp.float32)
    y = xf * lax.rsqrt(jnp.mean(xf * xf, axis=-1, keepdims=True) + EPS)
    return (y * g.astype(jnp.float32)).astype(x.dtype)


def stick_breaking_attention(q, k, v, q_pos, k_pos, bias):
    B, Tq, H, Dh = q.shape
    blk = SB_QBLOCK if Tq % SB_QBLOCK == 0 else Tq
    nb = Tq // blk
    qb = q.astype(jnp.float32).reshape(B, nb, blk, H, Dh).transpose(1, 0, 2, 3, 4)
    pb = q_pos.reshape(nb, blk)
    kf = k.astype(jnp.float32)
    vf = v.astype(jnp.float32)
    bf = bias.astype(jnp.float32)[None, :, None, None]
    scale = 1.0 / math.sqrt(Dh)

    def one_block(args):
        qi, pi = args
        z = jnp.einsum("bqhd,bkhd->bhqk", qi, kf) * scale + bf
        mask = (k_pos[None, :] < pi[:, None])[None, None]
        log_1mb = jnp.where(mask, jax.nn.log_sigmoid(-z), 0.0)
        suffix = lax.cumsum(log_1mb, axis=3, reverse=True) - log_1mb
        a = jnp.where(mask, jnp.exp(jax.nn.log_sigmoid(z) + suffix), 0.0)
        return jnp.einsum("bhqk,bkhd->bqhd", a, vf)

    o = lax.map(one_block, (qb, pb))
    return o.transpose(1, 0, 2, 3, 4).reshape(B, Tq, H, Dh).astype(q.dtype)


def multiscale_pool(u_ext, start_pos, pool_w, pool_scale):
    B, L, W = u_ext.shape
    T = L - POOL_STATE
    uf = u_ext.astype(jnp.float32)
    cs = jnp.concatenate([jnp.zeros((B, 1, W), jnp.float32), jnp.cumsum(uf, axis=1)], axis=1)
    pos = start_pos + jnp.arange(T, dtype=jnp.int32)
    cur = uf[:, POOL_STATE:]
    groups = []
    for g, w in enumerate(POOL_WINDOWS):
        c = slice(g * POOL_GROUP_DIM, (g + 1) * POOL_GROUP_DIM)
        win_sum = cs[:, POOL_STATE + 1:POOL_STATE + 1 + T, c] - cs[:, POOL_STATE + 1 - w:POOL_STATE + 1 - w + T, c]
        count = jnp.minimum(w, pos + 1).astype(jnp.float32)[None, :, None]
        groups.append(win_sum / count - cur[..., c])
    pooled = jnp.stack(groups, axis=2)
    mixed = jnp.einsum("btgc,gcd->btgd", pooled, pool_w.astype(jnp.float32)).reshape(B, T, W)
    return (mixed * pool_scale.astype(jnp.float32)).astype(u_ext.dtype)


def gla_chunked(q, k, v, log_a, s0):
    B, T, H, Dk = q.shape
    Dv = v.shape[-1]
    C = GLA_CHUNK if T % GLA_CHUNK == 0 else T
    n = T // C

    def chunks(a):
        return a.astype(jnp.float32).reshape(B, n, C, H, a.shape[-1]).transpose(1, 0, 3, 2, 4)

    causal = jnp.tril(jnp.ones((C, C), dtype=bool))[None, None, :, :, None]

    def step(s, inp):
        qi, ki, vi, ai = inp
        b = jnp.cumsum(ai, axis=2)
        o_inter = jnp.einsum("bhtd,bhde->bhte", qi * jnp.exp(b), s)
        decay = jnp.exp(jnp.where(causal, b[:, :, :, None, :] - b[:, :, None, :, :], -jnp.inf))
        scores = jnp.einsum("bhtd,bhsd,bhtsd->bhts", qi, ki, decay)
        o = o_inter + jnp.einsum("bhts,bhse->bhte", scores, vi)
        b_end = b[:, :, -1:, :]
        s_new = jnp.exp(b_end[:, :, 0, :, None]) * s + jnp.einsum("bhsd,bhse->bhde", ki * jnp.exp(b_end - b), vi)
        return s_new, o

    s_fin, o = lax.scan(step, s0.astype(jnp.float32), (chunks(q), chunks(k), chunks(v), chunks(log_a)))
    return o.transpose(1, 0, 3, 2, 4).reshape(B, T, H, Dv), s_fin


def mixer_layer(x, k_past, v_past, pool_prev, gla_prev, norm_g, w_in, sb_qnorm_g, sb_knorm_g, sb_bias,
                pool_w, pool_scale, gla_w2, gla_b2, gla_onorm_g, w_pa, w_pb, w_pc, w_o):
    B, T, _ = x.shape
    start = k_past.shape[1]
    h = rms_norm(x, norm_g)
    (sb_q, sb_k, sb_v, sb_g, pl_u, pl_g, gl_q, gl_k, gl_v, gl_g, gl_r, mg) = jnp.split(h @ w_in, _split_points(), axis=-1)

    q = rms_norm(sb_q.reshape(B, T, SB_HEADS, SB_HEAD_DIM), sb_qnorm_g)
    k = rms_norm(sb_k.reshape(B, T, SB_HEADS, SB_HEAD_DIM), sb_knorm_g)
    v = sb_v.reshape(B, T, SB_HEADS, SB_HEAD_DIM)
    k_all = jnp.concatenate([k_past.astype(k.dtype), k], axis=1)
    v_all = jnp.concatenate([v_past.astype(v.dtype), v], axis=1)
    q_pos = start + jnp.arange(T, dtype=jnp.int32)
    k_pos = jnp.arange(start + T, dtype=jnp.int32)
    o_a = stick_breaking_attention(q, k_all, v_all, q_pos, k_pos, sb_bias).reshape(B, T, SB_WIDTH) * jax.nn.silu(sb_g)

    u_ext = jnp.concatenate([pool_prev.astype(pl_u.dtype), pl_u], axis=1)
    o_b = multiscale_pool(u_ext, start, pool_w, pool_scale) * jax.nn.silu(pl_g)

    log_a = jax.nn.log_sigmoid((gl_r @ gla_w2 + gla_b2).astype(jnp.float32)) / GLA_TAU
    o_c, s_new = gla_chunked(gl_q.reshape(B, T, GLA_HEADS, GLA_DK) * (GLA_DK ** -0.5),
                             gl_k.reshape(B, T, GLA_HEADS, GLA_DK),
                             gl_v.reshape(B, T, GLA_HEADS, GLA_DV),
                             log_a.reshape(B, T, GLA_HEADS, GLA_DK), gla_prev)
    o_c = rms_norm(o_c, gla_onorm_g).reshape(B, T, GLA_VAL_WIDTH).astype(x.dtype) * jax.nn.silu(gl_g)

    gates = jax.nn.sigmoid(mg).reshape(B, T, N_BRANCH, D_MODEL)
    merged = gates[:, :, 0] * (o_a @ w_pa) + gates[:, :, 1] * (o_b @ w_pb) + gates[:, :, 2] * (o_c @ w_pc)
    y = x + merged @ w_o
    return y, k, v, u_ext[:, -POOL_STATE:], s_new.astype(gla_prev.dtype)


def setup_inputs(seed: int = 0) -> dict:
    key = jax.random.key(seed)
    ks = jax.random.split(key, 24)
    n_pages = PAST_LEN // PAGE_SIZE
    n_used = DEC_BATCH * n_pages
    n_pool = n_used + n_used // 4

    def nrm(k, shape, s=1.0):
        return s * jax.random.normal(k, shape, jnp.float32)

    page_table = jax.random.permutation(ks[0], n_pool)[:n_used].reshape(DEC_BATCH, n_pages).astype(jnp.int32)
    sb_bias = -jnp.linspace(SB_BIAS_MIN, SB_BIAS_MAX, SB_HEADS, dtype=jnp.float32)[None, :] + nrm(ks[20], (DEPTH, SB_HEADS), 0.1)
    return {
        "x_prompt": nrm(ks[1], (BATCH, SEQ, D_MODEL)),
        "x_sample": nrm(ks[2], (DEC_BATCH, DEC_SEQ, D_MODEL)),
        "cache_k": nrm(ks[3], (DEPTH, n_pool, PAGE_SIZE, SB_HEADS, SB_HEAD_DIM)),
        "cache_v": nrm(ks[4], (DEPTH, n_pool, PAGE_SIZE, SB_HEADS, SB_HEAD_DIM)),
        "state_pool": nrm(ks[5], (DEPTH, DEC_BATCH, POOL_STATE, POOL_WIDTH)),
        "state_gla": nrm(ks[6], (DEPTH, DEC_BATCH, GLA_HEADS, GLA_DK, GLA_DV), 0.5),
        "page_table": page_table,
        "norm_g": 1.0 + nrm(ks[7], (DEPTH, D_MODEL), 0.1),
        "w_in": nrm(ks[8], (DEPTH, D_MODEL, N_IN), D_MODEL ** -0.5),
        "sb_qnorm_g": 1.0 + nrm(ks[9], (DEPTH, SB_HEAD_DIM), 0.1),
        "sb_knorm_g": 1.0 + nrm(ks[10], (DEPTH, SB_HEAD_DIM), 0.1),
        "sb_bias": sb_bias,
        "pool_w": nrm(ks[11], (DEPTH, POOL_GROUPS, POOL_GROUP_DIM, POOL_GROUP_DIM), POOL_GROUP_DIM ** -0.5),
        "pool_scale": 1.0 + nrm(ks[12], (DEPTH, POOL_WIDTH), 0.1),
        "gla_w2": nrm(ks[13], (DEPTH, GLA_RANK, GLA_KEY_WIDTH), GLA_RANK ** -0.5),
        "gla_b2": nrm(ks[14], (DEPTH, GLA_KEY_WIDTH), 0.1),
        "gla_onorm_g": 1.0 + nrm(ks[15], (DEPTH, GLA_DV), 0.1),
        "w_pa": nrm(ks[16], (DEPTH, SB_WIDTH, D_MODEL), SB_WIDTH ** -0.5),
        "w_pb": nrm(ks[17], (DEPTH, POOL_WIDTH, D_MODEL), POOL_WIDTH ** -0.5),
        "w_pc": nrm(ks[18], (DEPTH, GLA_VAL_WIDTH, D_MODEL), GLA_VAL_WIDTH ** -0.5),
        "w_o": nrm(ks[19], (DEPTH, D_MODEL, D_MODEL), D_MODEL ** -0.5),
    }


def reference(x_prompt, x_sample, cache_k, cache_v, state_pool, state_gla, page_table, norm_g, w_in,
              sb_qnorm_g, sb_knorm_g, sb_bias, pool_w, pool_scale, gla_w2, gla_b2, gla_onorm_g, w_pa, w_pb, w_pc, w_o):
    bp = x_prompt.shape[0]
    bs, n_pages = page_table.shape
    past = n_pages * cache_k.shape[2]
    y_p, y_s = x_prompt, x_sample
    kp_l, vp_l, pp_l, gp_l = [], [], [], []
    ks_l, vs_l, ps_l, gs_l = [], [], [], []
    for l in range(DEPTH):
        weights = (norm_g[l], w_in[l], sb_qnorm_g[l], sb_knorm_g[l], sb_bias[l], pool_w[l], pool_scale[l],
                   gla_w2[l], gla_b2[l], gla_onorm_g[l], w_pa[l], w_pb[l], w_pc[l], w_o[l])
        empty = jnp.zeros((bp, 0, SB_HEADS, SB_HEAD_DIM), x_prompt.dtype)
        y_p, kp, vp, pp, gp = mixer_layer(
            y_p, empty, empty,
            jnp.zeros((bp, POOL_STATE, POOL_WIDTH), x_prompt.dtype),
            jnp.zeros((bp, GLA_HEADS, GLA_DK, GLA_DV), state_gla.dtype), *weights)
        k_past = cache_k[l][page_table].reshape(bs, past, SB_HEADS, SB_HEAD_DIM)
        v_past = cache_v[l][page_table].reshape(bs, past, SB_HEADS, SB_HEAD_DIM)
        y_s, ksm, vsm, psm, gsm = mixer_layer(y_s, k_past, v_past, state_pool[l], state_gla[l], *weights)
        kp_l.append(kp); vp_l.append(vp); pp_l.append(pp); gp_l.append(gp)
        ks_l.append(ksm); vs_l.append(vsm); ps_l.append(psm); gs_l.append(gsm)
    return (y_p, y_s, jnp.stack(kp_l), jnp.stack(vp_l), jnp.stack(pp_l), jnp.stack(gp_l),
            jnp.stack(ks_l), jnp.stack(vs_l), jnp.stack(ps_l), jnp.stack(gs_l))
```

```python
from contextlib import ExitStack
import numpy as np
import ml_dtypes
import concourse.bass as bass
import concourse.mybir as mybir
from concourse.bass_utils import run_bass_kernel_spmd

F32 = mybir.dt.float32
BF16 = mybir.dt.bfloat16
I32 = mybir.dt.int32
AF = mybir.ActivationFunctionType
ALU = mybir.AluOpType
AX = mybir.AxisListType

D = 1024
SEQ = 8192
NCORES = 2
NSG = 4
NS = 16 * NSG
NST = 4 * NS
NTOK = SEQ + NST
NSLOT = SEQ + NST
DEPTH = 2
NPAGE = 16
NPOOL = 2560
N_IN = 9232
EPS = 1e-6
POOL_W = (2, 4, 8, 16)
K_LIMIT = 10 ** 12


class Res:
    __slots__ = ("name", "w", "r")

    def __init__(self, name=""):
        self.name = name
        self.w = None
        self.r = {}


class Eng:
    def __init__(self, name, sem, is_pe=False):
        self.name = name
        self.sem = sem
        self.count = 0
        self.known = {}
        self.ops = []
        self.is_pe = is_pe
        self.slots = []
        self.slot_i = 0


class K:
    def __init__(self, nc, es):
        self.nc = nc
        self.es = es
        self.sem_objs = {}
        names = ["pe", "act", "dve", "pool", "sp"]
        self.eng = {}
        for n in names:
            s = es.enter_context(nc.semaphore("sem_" + n))
            self.eng[n] = Eng(n, s, is_pe=(n == "pe"))
        for q, nslot in (("sp", 12), ("pool", 4), ("act", 4)):
            for i in range(nslot):
                s = es.enter_context(nc.semaphore(f"dq_{q}_{i}"))
                self.eng[q].slots.append([s, 0])
        self.bar = es.enter_context(nc.semaphore("bar"))
        self.bar_n = 0
        self.all_res = []

    def res(self, name=""):
        r = Res(name)
        self.all_res.append(r)
        return r

    def _collect(self, e, reads, writes):
        waits = {}

        def need(ev):
            sem, val = ev
            if e.is_pe and sem is e.sem:
                return
            if e.known.get(id(sem), 0) >= val:
                return
            k = id(sem)
            if k not in waits or waits[k][1] < val:
                waits[k] = (sem, val)

        for r in reads:
            if r.w is not None:
                need(r.w)
        for w in writes:
            if w.w is not None:
                need(w.w)
            for ev in w.r.values():
                need(ev)
        return waits, need

    def _commit(self, e, waits, ev, reads, writes):
        for k, (sem, val) in waits.items():
            e.known[k] = val
        for r in reads:
            r.r[id(ev[0])] = ev
        for w in writes:
            w.w = ev
            w.r = {}

    def _lim(self):
        self.nrec = getattr(self, "nrec", 0) + 1
        return self.nrec > K_LIMIT

    def op(self, en, fn, reads=(), writes=()):
        if self._lim():
            return
        e = self.eng[en]
        waits, _ = self._collect(e, reads, writes)
        e.count += 1
        ev = (e.sem, e.count)
        self._commit(e, waits, ev, reads, writes)
        e.ops.append((list(waits.values()), fn, (e.sem, 1)))

    def dma(self, q, out, in_, reads=(), writes=(), **kw):
        if self._lim():
            return
        e = self.eng[q]
        slot = e.slots[e.slot_i]
        e.slot_i = (e.slot_i + 1) % len(e.slots)
        waits, need = self._collect(e, reads, writes)
        if slot[1] > 0:
            need((slot[0], slot[1] * 16))
        slot[1] += 1
        ev = (slot[0], slot[1] * 16)
        self._commit(e, waits, ev, reads, writes)
        e.ops.append((list(waits.values()), (lambda h, out=out, in_=in_, kw=kw: h.dma_start(out=out, in_=in_, **kw)),
                      (slot[0], 16)))

    def raw_dma(self, q, fn, reads=(), writes=()):
        if self._lim():
            return
        e = self.eng[q]
        slot = e.slots[e.slot_i]
        e.slot_i = (e.slot_i + 1) % len(e.slots)
        waits, need = self._collect(e, reads, writes)
        if slot[1] > 0:
            need((slot[0], slot[1] * 16))
        slot[1] += 1
        ev = (slot[0], slot[1] * 16)
        self._commit(e, waits, ev, reads, writes)
        e.ops.append((list(waits.values()), fn, (slot[0], 16)))

    def barrier(self):
        self.bar_n += 1
        target = self.bar_n * 5
        for n, e in self.eng.items():
            waits = []
            for s, uses in e.slots:
                if uses > 0 and e.known.get(id(s), 0) < uses * 16:
                    waits.append((s, uses * 16))
                    e.known[id(s)] = uses * 16
            if e.count > 0:
                waits.append((e.sem, e.count))
            e.ops.append((waits, "bar_inc", None))
            e.ops.append(([(self.bar, target)], None, None))
        for r in self.all_res:
            r.w = None
            r.r = {}

    def emit(self):
        nc = self.nc
        handles = {"pe": "tensor", "act": "scalar", "dve": "vector", "pool": "gpsimd", "sp": "sync"}
        with nc.Block() as block:
            for n, e in self.eng.items():
                ops = e.ops
                e.ops = []
                bar = self.bar

                def body(h, ops=ops, bar=bar):
                    for waits, fn, inc in ops:
                        for sem, val in waits:
                            h.wait_ge(sem, val)
                        if fn is None:
                            continue
                        if fn == "bar_inc":
                            h.sem_inc(bar, 1)
                            continue
                        ins = fn(h)
                        ins.then_inc(inc[0], inc[1])

                getattr(block, handles[n])(body)


class Buf:
    def __init__(self, k, t, name):
        self.t = t
        self.r = k.res(name)

    def __getitem__(self, idx):
        return self.t[idx]


class Ring:
    def __init__(self, bufs):
        self.bufs = bufs
        self.i = 0

    def next(self):
        b = self.bufs[self.i]
        self.i = (self.i + 1) % len(self.bufs)
        return b


_uid = [0]


def _uname(name):
    _uid[0] += 1
    return f"{name}_{_uid[0]}"


def sb(k, es, name, shape, dt):
    return Buf(k, es.enter_context(k.nc.sbuf_tensor(_uname(name), shape, dt)), name)


def ps(k, es, name, shape, dt):
    return Buf(k, es.enter_context(k.nc.psum_tensor(_uname(name), shape, dt)), name)


def sbring(k, es, name, shape, dt, n):
    return Ring([sb(k, es, f"{name}{i}", shape, dt) for i in range(n)])


def psring(k, es, name, shape, dt, n):
    return Ring([ps(k, es, f"{name}{i}", shape, dt) for i in range(n)])


def tile_list():
    return [(i * 128, 128) for i in range(64)] + [(SEQ + 64 * g, 64) for g in range(NSG)]


def phase_P(k, l, T):
    nc = k.nc
    with ExitStack() as es:
        ng_b = sb(k, es, "ng_b", [128, D], F32)
        qg_b = sb(k, es, "qg_b", [128, 64], F32)
        kg_b = sb(k, es, "kg_b", [128, 64], F32)
        og_b = sb(k, es, "og_b", [128, 256], F32)
        w2 = sb(k, es, "w2", [16, 512], BF16)
        nb2 = sb(k, es, "nb2", [128, 4], F32)
        ident = sb(k, es, "identP", [128, 128], BF16)
        hT = sb(k, es, "hT", [128, 8, 1024], BF16)
        Wring = sbring(k, es, "Wt", [128, 8, 512], BF16, 2)
        Wr = sb(k, es, "Wr", [128, 8, 16], BF16)
        xring = sbring(k, es, "xt", [128, D], F32, 2)
        junk = sb(k, es, "junkP", [128, D], BF16)
        hb = sbring(k, es, "hb", [128, D], BF16, 2)
        st = sbring(k, es, "stP", [128, 16], F32, 4)
        sq = sbring(k, es, "sqP", [128, 512], F32, 2)
        f32o = sbring(k, es, "f32o", [128, 512], F32, 4)
        bfo = sbring(k, es, "bfo", [128, 512], BF16, 4)
        rT = sb(k, es, "rT", [16, 1024], BF16)
        psT = ps(k, es, "psT", [128, 8, 128], BF16)
        psM = psring(k, es, "psM", [128, 512], F32, 4)
        psQ = psring(k, es, "psQ", [128, 8, 128], BF16, 2)

        k.dma("sp", ng_b[:, :], T["norm_g"][l:l + 1, :].to_broadcast([128, D]), writes=[ng_b.r])
        k.dma("sp", qg_b[:, :], T["sb_qnorm_g"][l:l + 1, :].to_broadcast([128, 64]), writes=[qg_b.r])
        k.dma("sp", kg_b[:, :], T["sb_knorm_g"][l:l + 1, :].to_broadcast([128, 64]), writes=[kg_b.r])
        k.dma("sp", og_b[:, :], T["gla_onorm_g"][l:l + 1, :].to_broadcast([128, 256]), writes=[og_b.r])
        k.dma("pool", w2[:, :], T["gla_w2"][l], writes=[w2.r])
        k.dma("sp", nb2[:, :], T["gla_b2"][l].rearrange("(c p) -> p c", p=128), writes=[nb2.r],
              allow_slow_non_contiguous=True)
        k.dma("sp", ident[:, :], T["c_ident"][:, :], writes=[ident.r])
        k.op("dve", lambda h: h.tensor_scalar(out=qg_b[:, :], in0=qg_b[:, :], scalar1=0.125, scalar2=None,
                                              op0=ALU.mult), reads=[qg_b.r], writes=[qg_b.r])
        k.op("dve", lambda h: h.tensor_scalar(out=nb2[:, :], in0=nb2[:, :], scalar1=-1.0, scalar2=None,
                                              op0=ALU.mult), reads=[nb2.r], writes=[nb2.r])

        w_in = T["w_in"][l].rearrange("(kc p) n -> p kc n", p=128)
        sblocks = [(i * 1024, 1024) for i in range(8)] + [(SEQ, NST)]
        evac_i = [0]

        def evac_engine():
            evac_i[0] += 1
            return "act" if evac_i[0] % 2 else "dve"

        for (t0, ntok) in sblocks:
            if t0 < SEQ:
                tiles = [(t0 + i * 128, 128) for i in range(ntok // 128)]
                halves = [(0, 512), (512, 512)]
            else:
                tiles = [(SEQ + 64 * g, 64) for g in range(NSG)]
                halves = [(0, NST)]
            for (tt, np_) in tiles:
                c0 = tt - t0
                xt = xringe = xring.next()
                src = T["x_p"][tt:tt + np_, :] if tt < SEQ else T["x_s"][tt - SEQ:tt - SEQ + np_, :]
                if l > 0:
                    src = T["Y"][tt:tt + np_, :]
                k.dma("sp", xt[0:np_, :], src, writes=[xt.r])
                s = st.next()
                k.op("act", lambda h, xt=xt, s=s, np_=np_: h.activation(out=junk[0:np_, :], in_=xt[0:np_, :], func=AF.Square,
                                                                         accum_out=s[0:np_, 0:1]),
                     reads=[xt.r], writes=[junk.r, s.r])
                k.op("act", lambda h, s=s, np_=np_: h.activation(out=s[0:np_, 1:2], in_=s[0:np_, 0:1], func=AF.Ln,
                                                                 scale=1.0 / D, bias=EPS), reads=[s.r], writes=[s.r])
                k.op("act", lambda h, s=s, np_=np_: h.activation(out=s[0:np_, 2:3], in_=s[0:np_, 1:2], func=AF.Exp,
                                                                 scale=-0.5), reads=[s.r], writes=[s.r])
                hbt = hb.next()
                k.op("dve", lambda h, xt=xt, s=s, hbt=hbt, np_=np_: h.scalar_tensor_tensor(
                    out=hbt[0:np_, :], in0=xt[0:np_, :], scalar=s[0:np_, 2:3], in1=ng_b[0:np_, :], op0=ALU.mult, op1=ALU.mult),
                    reads=[xt.r, s.r, ng_b.r], writes=[hbt.r])
                for kc in range(8):
                    k.op("pe", lambda h, hbt=hbt, kc=kc, np_=np_: h.transpose(out=psT[:, kc, 0:np_], in_=hbt[0:np_, kc * 128:(kc + 1) * 128],
                                                                               identity=ident[0:np_, 0:np_]),
                         reads=[hbt.r, ident.r], writes=[psT.r])
                k.op("act", lambda h, c0=c0, np_=np_: h.copy(out=hT[:, :, c0:c0 + np_], in_=psT[:, :, 0:np_]),
                     reads=[psT.r], writes=[hT.r])

            def load_w(c0):
                Wt = Wring.next()
                k.dma("pool", Wt[:, :, :], w_in[:, :, c0:c0 + 512], writes=[Wt.r])
                return Wt

            def tokmajor(Wt, tt, np_):
                c0 = tt - t0
                p = psM.next()
                for kc in range(8):
                    k.op("pe", lambda h, p=p, kc=kc, c0=c0, np_=np_, Wt=Wt: h.matmul(
                        out=p[0:np_, :], lhsT=hT[:, kc, c0:c0 + np_], rhs=Wt[:, kc, :], start=(kc == 0), stop=(kc == 7)),
                        reads=[hT.r, Wt.r], writes=[p.r])
                return p

            def featmajor(Wt, cc, h0, n, ncols=128, Wsl=None):
                p = psM.next()
                for kc in range(8):
                    lhs = Wt[:, kc, cc * 128:cc * 128 + ncols] if Wsl is None else Wsl(kc)
                    k.op("pe", lambda h, p=p, kc=kc, lhs=lhs, h0=h0, n=n, ncols=ncols: h.matmul(
                        out=p[0:ncols, 0:n], lhsT=lhs, rhs=hT[:, kc, h0:h0 + n], start=(kc == 0), stop=(kc == 7)),
                        reads=[hT.r, Wt.r], writes=[p.r])
                return p

            def qknorm(p, np_, gb, out_f32, out_bf):
                s_ = sq.next()
                k.op("act", lambda h: h.activation(out=s_[0:np_, :], in_=p[0:np_, :], func=AF.Square),
                     reads=[p.r], writes=[s_.r])
                s = st.next()
                k.op("dve", lambda h: h.tensor_reduce(out=s[0:np_, 0:8], in_=s_[0:np_, :].rearrange("p (a b) -> p a b", b=64),
                                                      axis=AX.X, op=ALU.add), reads=[s_.r], writes=[s.r])
                k.op("act", lambda h: h.activation(out=s[0:np_, 8:16], in_=s[0:np_, 0:8], func=AF.Ln, scale=1.0 / 64, bias=EPS),
                     reads=[s.r], writes=[s.r])
                k.op("act", lambda h: h.activation(out=s[0:np_, 0:8], in_=s[0:np_, 8:16], func=AF.Exp, scale=-0.5),
                     reads=[s.r], writes=[s.r])
                k.op("dve", lambda h: h.tensor_tensor(out=s_[0:np_, :].rearrange("p (a b) -> p a b", b=64),
                                                      in0=p[0:np_, :].rearrange("p (a b) -> p a b", b=64),
                                                      in1=s[0:np_, 0:8].unsqueeze(2).to_broadcast([np_, 8, 64]), op=ALU.mult),
                     reads=[p.r, s.r], writes=[s_.r])
                gbb = gb[0:np_, :].unsqueeze(1).to_broadcast([np_, 8, 64])
                if out_f32 is not None:
                    k.op("dve", lambda h: h.tensor_tensor(out=out_f32[0:np_, :].rearrange("p (a b) -> p a b", b=64),
                                                          in0=s_[0:np_, :].rearrange("p (a b) -> p a b", b=64), in1=gbb, op=ALU.mult),
                         reads=[s_.r, gb.r], writes=[out_f32.r])
                    k.op("pool", lambda h: h.tensor_copy(out=out_bf[0:np_, :], in_=out_f32[0:np_, :]),
                         reads=[out_f32.r], writes=[out_bf.r])
                else:
                    k.op("dve", lambda h: h.tensor_tensor(out=out_bf[0:np_, :].rearrange("p (a b) -> p a b", b=64),
                                                          in0=s_[0:np_, :].rearrange("p (a b) -> p a b", b=64), in1=gbb, op=ALU.mult),
                         reads=[s_.r, gb.r], writes=[out_bf.r])

            def to_featmajor_scratch(src_bf, np_, dst, tt):
                pq = psQ.next()
                for hp in range(4):
                    k.op("pe", lambda h, hp=hp: h.transpose(out=pq[:, hp, 0:np_], in_=src_bf[0:np_, hp * 128:(hp + 1) * 128],
                                                            identity=ident[0:np_, 0:np_]),
                         reads=[src_bf.r, ident.r], writes=[pq.r])
                o = bfo.next()
                k.op("act", lambda h: h.copy(out=o[:, 0:4 * np_].rearrange("p (a b) -> p a b", b=np_), in_=pq[:, 0:4, 0:np_]),
                     reads=[pq.r], writes=[o.r])
                k.dma("sp", dst.rearrange("(hp p) t -> p hp t", p=128)[:, :, tt:tt + np_],
                      o[:, 0:4 * np_].rearrange("p (a b) -> p a b", b=np_), reads=[o.r])

            def kv_out(name_p, name_s, tt, np_, srcbuf):
                if tt < SEQ:
                    k.dma("sp", T[name_p][l, tt:tt + np_, :], srcbuf[0:np_, :], reads=[srcbuf.r])
                else:
                    k.dma("sp", T[name_s][l, tt - SEQ:tt - SEQ + np_, :], srcbuf[0:np_, :], reads=[srcbuf.r])

            Wt = load_w(0)
            for (tt, np_) in tiles:
                p = tokmajor(Wt, tt, np_)
                ob = bfo.next()
                qknorm(p, np_, qg_b, None, ob)
                to_featmajor_scratch(ob, np_, T["QT"], tt)
            Wt = load_w(512)
            for (tt, np_) in tiles:
                p = tokmajor(Wt, tt, np_)
                of = f32o.next()
                ob = bfo.next()
                qknorm(p, np_, kg_b, of, ob)
                kv_out("k_p", "k_s", tt, np_, of)
                to_featmajor_scratch(ob, np_, T["KT"], tt)
            Wt = load_w(1024)
            for (tt, np_) in tiles:
                p = tokmajor(Wt, tt, np_)
                of = f32o.next()
                ob = bfo.next()
                k.op("act", lambda h, p=p, of=of, np_=np_: h.copy(out=of[0:np_, :], in_=p[0:np_, :]), reads=[p.r], writes=[of.r])
                k.op("dve", lambda h, of=of, ob=ob, np_=np_: h.tensor_copy(out=ob[0:np_, :], in_=of[0:np_, :]), reads=[of.r], writes=[ob.r])
                kv_out("v_p", "v_s", tt, np_, of)
                k.dma("sp", T["Vb"][tt:tt + np_, :], ob[0:np_, :], reads=[ob.r])

            def feat_group(c0, dst, func, scale=1.0):
                Wt = load_w(c0)
                for cc in range(4):
                    for (h0, n) in halves:
                        p = featmajor(Wt, cc, h0, n)
                        of = f32o.next()
                        if func is None:
                            eng = evac_engine()
                            if eng == "act":
                                k.op("act", lambda h, p=p, of=of, n=n: h.activation(out=of[:, 0:n], in_=p[:, 0:n], func=AF.Copy, scale=scale),
                                     reads=[p.r], writes=[of.r])
                            else:
                                k.op("dve", lambda h, p=p, of=of, n=n: h.tensor_scalar(out=of[:, 0:n], in0=p[:, 0:n], scalar1=scale, scalar2=None,
                                                                                       op0=ALU.mult), reads=[p.r], writes=[of.r])
                        else:
                            k.op("act", lambda h, p=p, of=of, n=n: h.activation(out=of[:, 0:n], in_=p[:, 0:n], func=func),
                                 reads=[p.r], writes=[of.r])
                        r0 = cc * 128
                        k.dma("sp", dst[r0:r0 + 128, t0 + h0:t0 + h0 + n], of[:, 0:n], reads=[of.r])
                return Wt

            feat_group(1536, T["SGT"], AF.Silu)
            Wt_u = feat_group(2048, T["UT"], None)
            if t0 == 7 * 1024:
                p = tokmajor(Wt_u, SEQ - 128, 128)
                of = f32o.next()
                k.op("act", lambda h, p=p, of=of: h.copy(out=of[:, :], in_=p[:, :]), reads=[p.r], writes=[of.r])
                k.dma("sp", T["pool_p"][l, :, :], of[113:128, :], reads=[of.r])
            if t0 == SEQ:
                for g in range(NSG):
                    p = tokmajor(Wt_u, SEQ + 64 * g, 64)
                    of = f32o.next()
                    k.op("act", lambda h, p=p, of=of: h.copy(out=of[0:64, :], in_=p[0:64, :]), reads=[p.r], writes=[of.r])
                    for b in range(16):
                        sq_ = 16 * g + b
                        k.dma("sp", T["pool_s"][l, sq_, 11:15, :], of[4 * b:4 * b + 4, :], reads=[of.r])
                        k.dma("sp", T["pool_s"][l, sq_, 0:11, :], T["state_pool"][l, sq_, 4:15, :])
            feat_group(2560, T["PGT"], AF.Silu)
            feat_group(3072, T["GQT"], None, scale=128.0 ** -0.5)
            feat_group(3584, T["GKT"], None)
            for blk in range(2):
                Wt = load_w(4096 + 512 * blk)
                for (tt, np_) in tiles:
                    p = tokmajor(Wt, tt, np_)
                    ob = bfo.next()
                    k.op(evac_engine() if False else "dve", lambda h, p=p, ob=ob, np_=np_: h.tensor_copy(out=ob[0:np_, :], in_=p[0:np_, :]),
                         reads=[p.r], writes=[ob.r])
                    k.dma("sp", T["GV"][tt:tt + np_, 512 * blk:512 * blk + 512], ob[0:np_, :], reads=[ob.r])
            for blk in range(2):
                Wt = load_w(5120 + 512 * blk)
                for (tt, np_) in tiles:
                    p = tokmajor(Wt, tt, np_)
                    of = f32o.next()
                    k.op("act", lambda h, p=p, of=of, np_=np_: h.activation(out=of[0:np_, :], in_=p[0:np_, :], func=AF.Silu),
                         reads=[p.r], writes=[of.r])
                    k.op("pool", lambda h, of=of, np_=np_: h.tensor_tensor(
                        out=of[0:np_, :].rearrange("p (a b) -> p a b", b=256), in0=of[0:np_, :].rearrange("p (a b) -> p a b", b=256),
                        in1=og_b[0:np_, :].unsqueeze(1).to_broadcast([np_, 2, 256]), op=ALU.mult),
                        reads=[of.r, og_b.r], writes=[of.r])
                    k.dma("sp", T["GG"][tt:tt + np_, 512 * blk:512 * blk + 512], of[0:np_, :], reads=[of.r])
            k.dma("pool", Wr[:, :, :], w_in[:, :, 6144:6160], writes=[Wr.r])
            for (h0, n) in halves:
                p = featmajor(Wr, 0, h0, n, ncols=16, Wsl=lambda kc: Wr[:, kc, :])
                k.op("act", lambda h, p=p, h0=h0, n=n: h.copy(out=rT[0:16, h0:h0 + n], in_=p[0:16, 0:n]), reads=[p.r], writes=[rT.r])
            for cc in range(4):
                for (h0, n) in halves:
                    p = psM.next()
                    k.op("pe", lambda h, p=p, cc=cc, h0=h0, n=n: h.matmul(out=p[:, 0:n], lhsT=w2[0:16, cc * 128:(cc + 1) * 128],
                                                                           rhs=rT[0:16, h0:h0 + n], start=True, stop=True),
                         reads=[w2.r, rT.r], writes=[p.r])
                    of = f32o.next()
                    k.op("act", lambda h, p=p, of=of, cc=cc, n=n: h.activation(out=of[:, 0:n], in_=p[:, 0:n], func=AF.Exp, scale=-1.0,
                                                                                bias=nb2[:, cc:cc + 1]), reads=[p.r, nb2.r], writes=[of.r])
                    k.op("act", lambda h, of=of, n=n: h.activation(out=of[:, 0:n], in_=of[:, 0:n], func=AF.Ln, scale=1.0, bias=1.0),
                         reads=[of.r], writes=[of.r])
                    k.op("dve", lambda h, of=of, n=n: h.tensor_scalar(out=of[:, 0:n], in0=of[:, 0:n], scalar1=-1.0 / 16.0, scalar2=None,
                                                                      op0=ALU.mult), reads=[of.r], writes=[of.r])
                    k.dma("sp", T["LAT"][cc * 128:(cc + 1) * 128, t0 + h0:t0 + h0 + n], of[:, 0:n], reads=[of.r])
            for blk in range(6):
                Wt = load_w(6160 + 512 * blk)
                for cc in range(4):
                    for (h0, n) in halves:
                        p = featmajor(Wt, cc, h0, n)
                        of = f32o.next()
                        k.op("act", lambda h, p=p, of=of, n=n: h.activation(out=of[:, 0:n], in_=p[:, 0:n], func=AF.Sigmoid),
                             reads=[p.r], writes=[of.r])
                        r0 = blk * 512 + cc * 128
                        k.dma("sp", T["MGT"][r0:r0 + 128, t0 + h0:t0 + h0 + n], of[:, 0:n], reads=[of.r])
        k.barrier()
        k.emit()


def phase_A(k, l, T):
    with ExitStack() as es:
        KTp = sb(k, es, "KTp", [128, SEQ], BF16)
        Vp = sb(k, es, "Vp", [128, 64, 128], BF16)
        QTq = sbring(k, es, "QTq", [128, 512], BF16, 2)
        SGq = sbring(k, es, "SGq", [128, 512], F32, 2)
        bias = sb(k, es, "biasA", [128, 8], F32)
        tri = sb(k, es, "triA", [128, 128], BF16)
        comp = sb(k, es, "compA", [128, 128], BF16)
        maskd = sb(k, es, "maskd", [128, 4, 512], F32)
        Er = sbring(k, es, "E", [128, 512], F32, 4)
        SPr = sbring(k, es, "SP", [128, 512], BF16, 4)
        Wr_ = sbring(k, es, "Wa", [128, 512], F32, 3)
        Ar = sbring(k, es, "Aa", [128, 512], BF16, 3)
        oa = sbring(k, es, "oa", [128, 512], BF16, 2)
        Z = psring(k, es, "Z", [128, 512], F32, 4)
        ACC = [ps(k, es, f"ACC{e}", [128, 512], F32) for e in range(2)]
        O = ps(k, es, "O", [128, 512], F32)

        k.dma("sp", bias[:, :], T["sb_bias"][l:l + 1, :].to_broadcast([128, 8]), writes=[bias.r])
        k.dma("sp", tri[:, :], T["c_tri"][:, :], writes=[tri.r])
        k.dma("sp", comp[:, :], T["c_comp"][:, :], writes=[comp.r])
        k.dma("sp", maskd[:, :, :], T["c_maskd"][:, :, :], writes=[maskd.r])

        def block_step(e, Zt, ncol, bias_ap, mask_ap, acc, Vl, Oout, first, a_dt_ring=Ar):
            Et = Er.next()
            k.op("act", lambda h: h.activation(out=Et[:, 0:ncol], in_=Zt[:, 0:ncol], func=AF.Exp, bias=bias_ap, scale=1.0),
                 reads=[Zt.r, bias.r], writes=[Et.r])
            if mask_ap is not None:
                k.op("pool", lambda h: h.tensor_tensor(out=Et[:, 0:ncol], in0=Et[:, 0:ncol], in1=mask_ap, op=ALU.mult),
                     reads=[Et.r, maskd.r], writes=[Et.r])
            SPt = SPr.next()
            k.op("act", lambda h: h.activation(out=SPt[:, 0:ncol], in_=Et[:, 0:ncol], func=AF.Ln, bias=1.0, scale=1.0),
                 reads=[Et.r], writes=[SPt.r])
            if first:
                k.op("dve", lambda h: h.memset(acc[:, 0:ncol], 0.0), writes=[acc.r])
            k.op("pe", lambda h: h.matmul(out=acc[:, 0:ncol], lhsT=tri[:, :], rhs=SPt[:, 0:ncol], start=False, stop=True,
                                          skip_group_check=True), reads=[tri.r, SPt.r, acc.r], writes=[acc.r])
            Wt = Wr_.next()
            k.op("act", lambda h: h.activation(out=Wt[:, 0:ncol], in_=acc[:, 0:ncol], func=AF.Exp, scale=-1.0),
                 reads=[acc.r], writes=[Wt.r])
            k.op("pe", lambda h: h.matmul(out=acc[:, 0:ncol], lhsT=comp[:, :], rhs=SPt[:, 0:ncol], start=False, stop=True,
                                          skip_group_check=True), reads=[comp.r, SPt.r, acc.r], writes=[acc.r])
            At = Ar.next()
            k.op("dve", lambda h: h.tensor_tensor(out=At[:, 0:ncol], in0=Et[:, 0:ncol], in1=Wt[:, 0:ncol], op=ALU.mult),
                 reads=[Et.r, Wt.r], writes=[At.r])
            return At

        for hp in range(4):
            k.dma("sp", KTp[:, :], T["KT"][hp * 128:(hp + 1) * 128, 0:SEQ], writes=[KTp.r])
            k.dma("sp", Vp[:, :, :], T["Vb"][0:SEQ, :].rearrange("(j p) c -> p j c", p=128)[:, :, hp * 128:(hp + 1) * 128],
                  writes=[Vp.r])
            for I in range(16):
                q0 = 512 * I
                Qt = QTq.next()
                Sg = SGq.next()
                k.dma("sp", Qt[:, :], T["QT"][hp * 128:(hp + 1) * 128, q0:q0 + 512], writes=[Qt.r])
                k.dma("sp", Sg[:, :], T["SGT"][hp * 128:(hp + 1) * 128, q0:q0 + 512], writes=[Sg.r])
                k.op("dve", lambda h: h.memset(O[:, :], 0.0), writes=[O.r])
                jlast = 4 * I + 3
                for j in range(jlast, -1, -1):
                    for e in range(2):
                        Zt = Z.next()
                        k.op("pe", lambda h, Zt=Zt, e=e, j=j, Qt=Qt: h.matmul(
                            out=Zt[:, :], lhsT=KTp[64 * e:64 * e + 64, 128 * j:128 * j + 128], rhs=Qt[64 * e:64 * e + 64, :],
                            start=True, stop=True), reads=[KTp.r, Qt.r], writes=[Zt.r])
                        hd = 2 * hp + e
                        mask_ap = maskd[:, j - 4 * I, :] if j >= 4 * I else None
                        At = block_step(e, Zt, 512, bias[:, hd:hd + 1], mask_ap, ACC[e], None, None, first=(j == jlast))
                        k.op("pe", lambda h, At=At, e=e, j=j: h.matmul(
                            out=O[64 * e:64 * e + 64, :], lhsT=Vp[:, j, 64 * e:64 * e + 64], rhs=At[:, :], start=False, stop=True,
                            skip_group_check=True), reads=[Vp.r, At.r, O.r], writes=[O.r])
                ot = oa.next()
                k.op("dve", lambda h, ot=ot, Sg=Sg: h.tensor_tensor(out=ot[:, :], in0=O[:, :], in1=Sg[:, :], op=ALU.mult),
                     reads=[O.r, Sg.r], writes=[ot.r])
                k.dma("sp", T["OAT"][hp * 128:(hp + 1) * 128, q0:q0 + 512], ot[:, :], reads=[ot.r])

        ident = sb(k, es, "identA", [128, 128], BF16)
        k.dma("sp", ident[:, :], T["c_ident"][:, :], writes=[ident.r])
        maskn = sb(k, es, "maskn", [128, 32], F32)
        k.dma("sp", maskn[:, :], T["c_maskn"][:, :], writes=[maskn.r])
        biasf = sb(k, es, "biasf", [128, 8, 4], F32)
        k.op("dve", lambda h: h.tensor_copy(out=biasf[:, :, :], in_=bias[:, :].unsqueeze(2).to_broadcast([128, 8, 4])),
             reads=[bias.r], writes=[biasf.r])
        pt_i = sb(k, es, "pt_i", [128, NS * NPAGE], I32)
        iota_f = sb(k, es, "iota_f", [128, 1], F32)
        idx = sb(k, es, "idx", [128, NS * NPAGE], I32)
        k.dma("sp", pt_i[:, :], T["page_table"].rearrange("b j -> (b j)").unsqueeze(0).to_broadcast([128, NS * NPAGE]),
              writes=[pt_i.r])
        k.dma("sp", iota_f[:, :], T["c_iota"][:, :], writes=[iota_f.r])
        idxf = sb(k, es, "idxf", [128, NS * NPAGE], F32)
        k.op("dve", lambda h: h.tensor_scalar(out=idxf[:, :], in0=pt_i[:, :], scalar1=128.0, scalar2=iota_f[:, 0:1], op0=ALU.mult,
                                              op1=ALU.add), reads=[pt_i.r, iota_f.r], writes=[idxf.r])
        k.op("dve", lambda h: h.tensor_scalar(out=idx[:, :], in0=idxf[:, :], scalar1=float(l * NPOOL * 128), scalar2=None,
                                              op0=ALU.add), reads=[idxf.r], writes=[idx.r])
        QS = sb(k, es, "QS", [128, 4, NST], BF16)
        KN = sb(k, es, "KN", [128, 4, NST], BF16)
        SGs = sb(k, es, "SGs", [128, 4, NST], F32)
        VN = sb(k, es, "VN", [128, 512], BF16)
        k.dma("sp", QS[:, :, :], T["QT"].rearrange("(hp p) t -> p hp t", p=128)[:, :, SEQ:SEQ + NST], writes=[QS.r])
        k.dma("sp", KN[:, :, :], T["KT"].rearrange("(hp p) t -> p hp t", p=128)[:, :, SEQ:SEQ + NST], writes=[KN.r])
        k.dma("sp", SGs[:, :, :], T["SGT"].rearrange("(hp p) t -> p hp t", p=128)[:, :, SEQ:SEQ + NST], writes=[SGs.r])
        Kf = sbring(k, es, "Kf", [128, 512], F32, 3)
        Vf = sbring(k, es, "Vf", [128, 512], F32, 3)
        Kb = sbring(k, es, "Kb", [128, 512], BF16, 2)
        Vb_ = sbring(k, es, "Vbs", [128, 512], BF16, 3)
        KTs = sbring(k, es, "KTs", [128, 4, 128], BF16, 2)
        zb = sbring(k, es, "zb", [128, 32], F32, 2)
        oas = sb(k, es, "oas", [128, 4, NST], BF16)
        psK = psring(k, es, "psK", [128, 8, 128], BF16, 1)
        ck = T["cache_k"].rearrange("l r c -> (l r) c")
        cv = T["cache_v"].rearrange("l r c -> (l r) c")
        k.op("dve", lambda h: h.memset(VN[:, :], 0.0), writes=[VN.r])
        Os = O
        Qb = sbring(k, es, "Qb", [128, 4, 16], BF16, 2)
        Knb = sbring(k, es, "Knb", [128, 4, 128], BF16, 2)
        for kb_ in Knb.bufs:
            k.op("pool", lambda h, kb_=kb_: h.memset(kb_[:, :, :], 0.0), writes=[kb_.r])
        for b in range(NS):
            k.dma("sp", VN[0:4, :], T["Vb"][SEQ + 4 * b:SEQ + 4 * b + 4, :], writes=[VN.r])
            qb_ = Qb.next()
            knb_ = Knb.next()
            k.op("pool", lambda h, qb_=qb_, b=b: h.tensor_copy(out=qb_[:, :, 0:4], in_=QS[:, :, 4 * b:4 * b + 4]), reads=[QS.r], writes=[qb_.r])
            k.op("pool", lambda h, knb_=knb_, b=b: h.tensor_copy(out=knb_[:, :, 0:4], in_=KN[:, :, 4 * b:4 * b + 4]), reads=[KN.r], writes=[knb_.r])
            k.op("dve", lambda h: h.memset(Os[:, 0:16], 0.0), writes=[Os.r])
            acc = ACC[b % 2]
            for blk in range(NPAGE, -1, -1):
                Zt = Z.next()
                if blk == NPAGE:
                    for hd in range(8):
                        hp, e = hd // 2, hd % 2
                        k.op("pe", lambda h, Zt=Zt, hp=hp, e=e, hd=hd, knb_=knb_, qb_=qb_: h.matmul(
                            out=Zt[:, 4 * hd:4 * hd + 4], lhsT=knb_[64 * e:64 * e + 64, hp, :],
                            rhs=qb_[64 * e:64 * e + 64, hp, 0:4], start=True, stop=True, skip_group_check=True),
                            reads=[knb_.r, qb_.r], writes=[Zt.r])
                    z_ = zb.next()
                    k.op("dve", lambda h, Zt=Zt, z_=z_: h.tensor_tensor(out=z_[:, :], in0=Zt[:, 0:32],
                                                                       in1=biasf[:, :, :].rearrange("p a b -> p (a b)"), op=ALU.add),
                         reads=[Zt.r, biasf.r], writes=[z_.r])
                    Et = Er.next()
                    k.op("act", lambda h, Et=Et, z_=z_: h.activation(out=Et[:, 0:32], in_=z_[:, :], func=AF.Exp),
                         reads=[z_.r], writes=[Et.r])
                    k.op("pool", lambda h, Et=Et: h.tensor_tensor(out=Et[:, 0:32], in0=Et[:, 0:32], in1=maskn[:, :], op=ALU.mult),
                         reads=[Et.r, maskn.r], writes=[Et.r])
                    Vl = VN
                else:
                    col = b * NPAGE + blk
                    kf = Kf.next()
                    vf = Vf.next()
                    k.raw_dma("pool", lambda h, kf=kf, col=col: h.indirect_dma_start(
                        out=kf[:, :], out_offset=None, in_=ck, in_offset=bass.IndirectOffsetOnAxis(ap=idx[:, col:col + 1], axis=0)),
                        reads=[idx.r], writes=[kf.r])
                    k.raw_dma("pool", lambda h, vf=vf, col=col: h.indirect_dma_start(
                        out=vf[:, :], out_offset=None, in_=cv, in_offset=bass.IndirectOffsetOnAxis(ap=idx[:, col:col + 1], axis=0)),
                        reads=[idx.r], writes=[vf.r])
                    kb = Kb.next()
                    vb = Vb_.next()
                    k.op("dve", lambda h, kb=kb, kf=kf: h.tensor_copy(out=kb[:, :], in_=kf[:, :]), reads=[kf.r], writes=[kb.r])
                    k.op("pool", lambda h, vb=vb, vf=vf: h.tensor_copy(out=vb[:, :], in_=vf[:, :]), reads=[vf.r], writes=[vb.r])
                    pk = psK.next()
                    for hp in range(4):
                        k.op("pe", lambda h, pk=pk, kb=kb, hp=hp: h.transpose(out=pk[:, hp, :], in_=kb[:, hp * 128:(hp + 1) * 128],
                                                                             identity=ident[:, :]),
                             reads=[kb.r, ident.r], writes=[pk.r])
                    kt = KTs.next()
                    k.op("act", lambda h, kt=kt, pk=pk: h.copy(out=kt[:, :, :], in_=pk[:, 0:4, :]), reads=[pk.r], writes=[kt.r])
                    for hd in range(8):
                        hp, e = hd // 2, hd % 2
                        k.op("pe", lambda h, Zt=Zt, kt=kt, hp=hp, e=e, hd=hd, qb_=qb_: h.matmul(
                            out=Zt[:, 4 * hd:4 * hd + 4], lhsT=kt[64 * e:64 * e + 64, hp, :],
                            rhs=qb_[64 * e:64 * e + 64, hp, 0:4], start=True, stop=True, skip_group_check=True),
                            reads=[kt.r, qb_.r], writes=[Zt.r])
                    z_ = zb.next()
                    k.op("dve", lambda h, Zt=Zt, z_=z_: h.tensor_tensor(out=z_[:, :], in0=Zt[:, 0:32],
                                                                       in1=biasf[:, :, :].rearrange("p a b -> p (a b)"), op=ALU.add),
                         reads=[Zt.r, biasf.r], writes=[z_.r])
                    Et = Er.next()
                    k.op("act", lambda h, Et=Et, z_=z_: h.activation(out=Et[:, 0:32], in_=z_[:, :], func=AF.Exp),
                         reads=[z_.r], writes=[Et.r])
                    Vl = vb
                SPt = SPr.next()
                k.op("act", lambda h, SPt=SPt, Et=Et: h.activation(out=SPt[:, 0:32], in_=Et[:, 0:32], func=AF.Ln, bias=1.0, scale=1.0),
                     reads=[Et.r], writes=[SPt.r])
                if blk == NPAGE:
                    k.op("dve", lambda h, acc=acc: h.memset(acc[:, 0:32], 0.0), writes=[acc.r])
                k.op("pe", lambda h, acc=acc, SPt=SPt: h.matmul(out=acc[:, 0:32], lhsT=tri[:, :], rhs=SPt[:, 0:32], start=False, stop=True,
                                                                skip_group_check=True), reads=[tri.r, SPt.r, acc.r], writes=[acc.r])
                Wt = Wr_.next()
                k.op("act", lambda h, acc=acc, Wt=Wt: h.activation(out=Wt[:, 0:32], in_=acc[:, 0:32], func=AF.Exp, scale=-1.0),
                     reads=[acc.r], writes=[Wt.r])
                k.op("pe", lambda h, acc=acc, SPt=SPt: h.matmul(out=acc[:, 0:32], lhsT=comp[:, :], rhs=SPt[:, 0:32], start=False, stop=True,
                                                                skip_group_check=True), reads=[comp.r, SPt.r, acc.r], writes=[acc.r])
                At = Ar.next()
                k.op("dve", lambda h, At=At, Et=Et, Wt=Wt: h.tensor_tensor(out=At[:, 0:32], in0=Et[:, 0:32], in1=Wt[:, 0:32], op=ALU.mult),
                     reads=[Et.r, Wt.r], writes=[At.r])
                for hd in range(8):
                    hp, e = hd // 2, hd % 2
                    k.op("pe", lambda h, At=At, Vl=Vl, hp=hp, e=e, hd=hd: h.matmul(
                        out=Os[64 * e:64 * e + 64, 4 * hp:4 * hp + 4], lhsT=Vl[:, 64 * hd:64 * hd + 64], rhs=At[:, 4 * hd:4 * hd + 4],
                        start=False, stop=True, skip_group_check=True), reads=[Vl.r, At.r, Os.r], writes=[Os.r])
            k.op("dve", lambda h, b=b: h.tensor_tensor(out=oas[:, :, 4 * b:4 * b + 4], in0=Os[:, 0:16].rearrange("p (a b) -> p a b", b=4),
                                                      in1=SGs[:, :, 4 * b:4 * b + 4], op=ALU.mult),
                 reads=[Os.r, SGs.r], writes=[oas.r])
        k.dma("sp", T["OAT"].rearrange("(hp p) t -> p hp t", p=128)[:, :, SEQ:SEQ + NST], oas[:, :, :], reads=[oas.r])
        k.barrier()
        k.emit()


def phase_B(k, l, T):
    with ExitStack() as es:
        pw = sb(k, es, "pw", [128, 4, 128], BF16)
        psc = sb(k, es, "psc", [128, 4], F32)
        invc = sb(k, es, "invc", [128, 4, 128], F32)
        k.dma("pool", pw[:, :, :], T["pool_w"][l].rearrange("g c d -> c g d"), writes=[pw.r])
        k.dma("sp", psc[:, :], T["pool_scale"][l].rearrange("(g p) -> p g", p=128), writes=[psc.r], allow_slow_non_contiguous=True)
        k.dma("sp", invc[:, :, :], T["c_invc"][:, :, :], writes=[invc.r])
        LP = 16 + 512
        Ur = sbring(k, es, "U", [128, LP], F32, 2)
        S1 = sbring(k, es, "S1", [128, LP], F32, 2)
        S2 = sbring(k, es, "S2", [128, LP], F32, 2)
        PGr = sbring(k, es, "PGb", [128, 512], F32, 2)
        pl = sbring(k, es, "plb", [128, 512], BF16, 2)
        ob = sbring(k, es, "obB", [128, 512], BF16, 2)
        pm = psring(k, es, "pmB", [128, 512], F32, 2)
        ident = sb(k, es, "identB", [128, 128], F32)
        k.dma("sp", ident[:, :], T["c_identf"][:, :], writes=[ident.r])

        def pool_block(g, U, G, L, first, pg_ap, pg_res, out_dst):
            Tn = L - 16
            n = G * Tn
            w = POOL_W[g]
            v3 = lambda buf: buf[:, 0:G * L].rearrange("p (g l) -> p g l", l=L)
            cur = U
            sh = 1
            lo = 0
            bufs = [S1.next(), S2.next()]
            bi = 0
            while sh < w:
                nxt = bufs[bi]
                bi ^= 1
                nlo = lo + sh
                k.op("pool", lambda h, cur=cur, nxt=nxt, nlo=nlo, sh=sh: h.tensor_tensor(
                    out=v3(nxt)[:, :, nlo:L], in0=v3(cur)[:, :, nlo:L], in1=v3(cur)[:, :, nlo - sh:L - sh], op=ALU.add),
                    reads=[cur.r], writes=[nxt.r])
                cur = nxt
                lo = nlo
                sh *= 2
            p_ = pl.next()
            pv = p_[:, 0:n].rearrange("p (g t) -> p g t", t=Tn)
            if first:
                k.op("dve", lambda h: h.tensor_tensor(out=v3(cur)[:, :, 16:L], in0=v3(cur)[:, :, 16:L],
                                                      in1=invc[:, g, 0:Tn].unsqueeze(1).to_broadcast([128, G, Tn]), op=ALU.mult),
                     reads=[cur.r, invc.r], writes=[cur.r])
                k.op("dve", lambda h: h.tensor_tensor(out=pv, in0=v3(cur)[:, :, 16:L], in1=v3(U)[:, :, 16:L], op=ALU.subtract),
                     reads=[cur.r, U.r], writes=[p_.r])
            else:
                k.op("dve", lambda h: h.scalar_tensor_tensor(out=pv, in0=v3(cur)[:, :, 16:L], scalar=1.0 / w, in1=v3(U)[:, :, 16:L],
                                                             op0=ALU.mult, op1=ALU.subtract), reads=[cur.r, U.r], writes=[p_.r])
            pm_ = pm.next()
            k.op("pe", lambda h: h.matmul(out=pm_[:, 0:n], lhsT=pw[:, g, :], rhs=p_[:, 0:n], start=True, stop=True),
                 reads=[pw.r, p_.r], writes=[pm_.r])
            o_ = ob.next()
            k.op("dve", lambda h: h.scalar_tensor_tensor(out=o_[:, 0:n], in0=pm_[:, 0:n], scalar=psc[:, g:g + 1], in1=pg_ap,
                                                         op0=ALU.mult, op1=ALU.mult), reads=[pm_.r, psc.r, pg_res], writes=[o_.r])
            k.dma("sp", out_dst, o_[:, 0:n], reads=[o_.r])

        for I in range(16):
            t0 = 512 * I
            for g in range(4):
                U = Ur.next()
                rows = slice(g * 128, (g + 1) * 128)
                if I == 0:
                    k.op("pool", lambda h, U=U: h.memset(U[:, 0:16], 0.0), writes=[U.r])
                    k.dma("sp", U[:, 16:LP], T["UT"][rows, 0:512], writes=[U.r])
                else:
                    k.dma("sp", U[:, :], T["UT"][rows, t0 - 16:t0 + 512], writes=[U.r])
                pg = PGr.next()
                k.dma("sp", pg[:, :], T["PGT"][rows, t0:t0 + 512], writes=[pg.r])
                if I == 0:
                    pool_block(g, U, 1, 16 + 128, True, pg[:, 0:128], pg.r, T["OBT"][rows, 0:128])
                    U2 = Ur.next()
                    k.dma("sp", U2[:, 0:16 + 384], T["UT"][rows, 128 - 16:512], writes=[U2.r])
                    pool_block(g, U2, 1, 16 + 384, False, pg[:, 128:512], pg.r, T["OBT"][rows, 128:512])
                else:
                    pool_block(g, U, 1, LP, False, pg[:, :], pg.r, T["OBT"][rows, t0:t0 + 512])

        spf = sbring(k, es, "spf", [120, 512], F32, 2)
        pst = psring(k, es, "pstB", [128, 512], F32, 2)
        for sgi in range(NSG):
            for g in range(4):
                rows = slice(g * 128, (g + 1) * 128)
                U = Ur.next()
                U3 = U[:, 0:16 * 20].rearrange("p (b l) -> p b l", l=20)
                k.op("pool", lambda h, U=U: h.memset(U[:, 0:16 * 20], 0.0), writes=[U.r])
                for half in range(2):
                    sp_ = spf.next()
                    s0_ = 16 * sgi + 8 * half
                    k.dma("sp", sp_[:, :], T["state_pool"][l, s0_:s0_ + 8, :, :].rearrange("b r c -> (b r) c"), writes=[sp_.r])
                    pt = pst.next()
                    k.op("pe", lambda h, pt=pt, sp_=sp_, g=g: h.transpose(out=pt[:, 0:120], in_=sp_[:, g * 128:(g + 1) * 128],
                                                                         identity=ident[0:120, 0:120]),
                         reads=[sp_.r, ident.r], writes=[pt.r])
                    k.op("act", lambda h, pt=pt, U3=U3, half=half: h.copy(out=U3[:, 8 * half:8 * half + 8, 1:16],
                                                                          in_=pt[:, 0:120].rearrange("p (b r) -> p b r", r=15)),
                         reads=[pt.r], writes=[U.r])
                c0_ = SEQ + 64 * sgi
                k.dma("sp", U3[:, :, 16:20], T["UT"][rows, c0_:c0_ + 64].rearrange("p (b t) -> p b t", t=4), writes=[U.r])
                pg = PGr.next()
                k.dma("sp", pg[:, 0:64], T["PGT"][rows, c0_:c0_ + 64], writes=[pg.r])
                pool_block(g, U, 16, 20, False, pg[:, 0:64], pg.r, T["OBT"][rows, c0_:c0_ + 64])
        k.barrier()
        k.emit()


def phase_C(k, l, T):
    with ExitStack() as es:
        ident = sb(k, es, "identC", [128, 128], BF16)
        k.dma("sp", ident[:, :], T["c_ident"][:, :], writes=[ident.r])
        cst = {}
        for tag, NT, NC in (("p", 128, 2), ("s", 64, 16)):
            SM = sb(k, es, "SM" + tag, [NT, NT], F32)
            CH = sb(k, es, "CH" + tag, [128, NC, NT], F32)
            RM = sb(k, es, "RM" + tag, [NT, NC], F32)
            RS = sb(k, es, "RS" + tag, [128, NT], F32)
            k.dma("sp", SM[:, :], T["c_sm_" + tag][:, :], writes=[SM.r])
            k.dma("sp", CH[:, :, :], T["c_ch_" + tag][:, :, :], writes=[CH.r])
            k.dma("sp", RM[:, :], T["c_rm_" + tag][:, :], writes=[RM.r])
            k.dma("sp", RS[:, :], T["c_rs_" + tag][:, :], writes=[RS.r])
            cst[tag] = (SM, CH, RM, RS)
        S = sb(k, es, "S", [128, 4, 256], F32)
        Sbf = sb(k, es, "Sbf", [128, 4, 256], BF16)
        k.op("dve", lambda h: h.memset(S[:, :, :], 0.0), writes=[S.r])
        k.op("pool", lambda h: h.memset(Sbf[:, :, :], 0.0), writes=[Sbf.r])
        gq = sbring(k, es, "gq", [128, 128], F32, 2)
        gk = sbring(k, es, "gk", [128, 128], F32, 2)
        la = sbring(k, es, "la", [128, 128], F32, 2)
        vv = sbring(k, es, "vv", [128, 256], BF16, 2)
        gg = sbring(k, es, "gg", [128, 256], F32, 2)
        bT = sbring(k, es, "bT", [128, 128], F32, 2)
        eb = sbring(k, es, "eb", [128, 128], F32, 2)
        enb = sbring(k, es, "enb", [128, 128], F32, 2)
        ebe = sbring(k, es, "ebe", [128, 128], F32, 2)
        ebend = sbring(k, es, "ebend", [128, 16], F32, 2)
        qb = sbring(k, es, "qb", [128, 128], BF16, 2)
        kb = sbring(k, es, "kb", [128, 128], BF16, 2)
        ke = sbring(k, es, "ke", [128, 128], BF16, 2)
        kendz = sbring(k, es, "kendz", [128, 2048], BF16, 2)
        qbz = sbring(k, es, "qbz", [128, 1024], BF16, 2)
        scm = sbring(k, es, "scm", [128, 128], BF16, 2)
        st = sbring(k, es, "stC", [128, 4], F32, 2)
        junk = sb(k, es, "junkC", [128, 256], BF16)
        oc = sbring(k, es, "oc", [128, 256], BF16, 2)
        ocT = sbring(k, es, "ocT", [128, 2, 128], BF16, 2)
        S0 = sbring(k, es, "S0", [128, 256], F32, 3)
        S0b = sbring(k, es, "S0b", [128, 256], BF16, 3)
        S1 = sbring(k, es, "S1o", [128, 256], F32, 3)
        psKE = ps(k, es, "psKE", [128, 1024], BF16)
        psSC = ps(k, es, "psSC", [128, 512], F32)
        psO = psring(k, es, "psO", [128, 512], F32, 2)
        psSU = psring(k, es, "psSU", [128, 512], F32, 2)
        psOT = ps(k, es, "psOT", [128, 8, 128], BF16)

        for (tt, NT) in tile_list():
            samp = tt >= SEQ
            tag = "s" if samp else "p"
            C = 4 if samp else 64
            NC = NT // C
            SM, CH, RM, RS = cst[tag]
            for hh in range(4):
                rows = slice(hh * 128, (hh + 1) * 128)
                q_, k_, a_, v_, g_ = gq.next(), gk.next(), la.next(), vv.next(), gg.next()
                k.dma("sp", q_[:, 0:NT], T["GQT"][rows, tt:tt + NT], writes=[q_.r])
                k.dma("sp", k_[:, 0:NT], T["GKT"][rows, tt:tt + NT], writes=[k_.r])
                k.dma("sp", a_[:, 0:NT], T["LAT"][rows, tt:tt + NT], writes=[a_.r])
                k.dma("sp", v_[0:NT, :], T["GV"][tt:tt + NT, hh * 256:(hh + 1) * 256], writes=[v_.r])
                k.dma("sp", g_[0:NT, :], T["GG"][tt:tt + NT, hh * 256:(hh + 1) * 256], writes=[g_.r])
                b_ = bT.next()
                k.op("dve", lambda h, b_=b_, a_=a_, NT=NT, RS=RS: h.tensor_tensor_scan(
                    out=b_[:, 0:NT], data0=RS[:, 0:NT], data1=a_[:, 0:NT], initial=0.0, op0=ALU.mult, op1=ALU.add),
                    reads=[a_.r, RS.r], writes=[b_.r])
                b3 = b_[:, 0:NT].rearrange("p (c t) -> p c t", t=C)
                e1, e2, e3, e4 = eb.next(), enb.next(), ebe.next(), ebend.next()
                k.op("act", lambda h, e1=e1, b_=b_, NT=NT: h.activation(out=e1[:, 0:NT], in_=b_[:, 0:NT], func=AF.Exp),
                     reads=[b_.r], writes=[e1.r])
                k.op("act", lambda h, e2=e2, b_=b_, NT=NT: h.activation(out=e2[:, 0:NT], in_=b_[:, 0:NT], func=AF.Exp, scale=-1.0),
                     reads=[b_.r], writes=[e2.r])
                k.op("pool", lambda h, e3=e3, b3=b3, NT=NT, C=C, NC=NC: h.tensor_tensor(
                    out=e3[:, 0:NT].rearrange("p (c t) -> p c t", t=C), in0=b3[:, :, C - 1:C].to_broadcast([128, NC, C]), in1=b3,
                    op=ALU.subtract), reads=[b_.r], writes=[e3.r])
                k.op("act", lambda h, e3=e3, NT=NT: h.activation(out=e3[:, 0:NT], in_=e3[:, 0:NT], func=AF.Exp), reads=[e3.r], writes=[e3.r])
                k.op("act", lambda h, e4=e4, b3=b3, NC=NC, C=C: h.activation(out=e4[:, 0:NC].unsqueeze(2), in_=b3[:, :, C - 1:C], func=AF.Exp),
                     reads=[b_.r], writes=[e4.r])
                qb_, kb_, ke_ = qb.next(), kb.next(), ke.next()
                k.op("dve", lambda h, qb_=qb_, q_=q_, e1=e1, NT=NT: h.tensor_tensor(out=qb_[:, 0:NT], in0=q_[:, 0:NT], in1=e1[:, 0:NT], op=ALU.mult),
                     reads=[q_.r, e1.r], writes=[qb_.r])
                k.op("pool", lambda h, kb_=kb_, k_=k_, e2=e2, NT=NT: h.tensor_tensor(out=kb_[:, 0:NT], in0=k_[:, 0:NT], in1=e2[:, 0:NT], op=ALU.mult),
                     reads=[k_.r, e2.r], writes=[kb_.r])
                k.op("pool", lambda h, ke_=ke_, k_=k_, e3=e3, NT=NT: h.tensor_tensor(out=ke_[:, 0:NT], in0=k_[:, 0:NT], in1=e3[:, 0:NT], op=ALU.mult),
                     reads=[k_.r, e3.r], writes=[ke_.r])
                k.op("pe", lambda h, ke_=ke_, NT=NT: h.transpose(out=psKE[0:NT, 0:128], in_=ke_[:, 0:NT], identity=ident[:, :]),
                     reads=[ke_.r, ident.r], writes=[psKE.r])
                kz = kendz.next()
                kz3 = kz[0:NT, 0:NC * 128].rearrange("p (c d) -> p c d", d=128)
                k.op("dve", lambda h, kz3=kz3, NT=NT, NC=NC, RM=RM: h.tensor_tensor(
                    out=kz3, in0=psKE[0:NT, 0:128].unsqueeze(1).to_broadcast([NT, NC, 128]),
                    in1=RM[0:NT, 0:NC].unsqueeze(2).to_broadcast([NT, NC, 128]), op=ALU.mult),
                    reads=[psKE.r, RM.r], writes=[kz.r])
                qz = qbz.next()
                qz3 = qz[:, 0:NC * NT].rearrange("p (c t) -> p c t", t=NT)
                k.op("pool", lambda h, qz3=qz3, qb_=qb_, NT=NT, NC=NC, CH=CH: h.tensor_tensor(
                    out=qz3, in0=qb_[:, 0:NT].unsqueeze(1).to_broadcast([128, NC, NT]), in1=CH[:, :, :], op=ALU.mult),
                    reads=[qb_.r, CH.r], writes=[qz.r])
                k.op("pe", lambda h, kb_=kb_, qb_=qb_, NT=NT: h.matmul(out=psSC[0:NT, 0:NT], lhsT=kb_[:, 0:NT], rhs=qb_[:, 0:NT],
                                                                      start=True, stop=True), reads=[kb_.r, qb_.r], writes=[psSC.r])
                sc_ = scm.next()
                k.op("dve", lambda h, sc_=sc_, NT=NT, SM=SM: h.tensor_tensor(out=sc_[0:NT, 0:NT], in0=psSC[0:NT, 0:NT], in1=SM[0:NT, 0:NT],
                                                                             op=ALU.mult), reads=[psSC.r, SM.r], writes=[sc_.r])
                O_ = psO.next()
                k.op("pe", lambda h, O_=O_, sc_=sc_, v_=v_, NT=NT: h.matmul(out=O_[0:NT, 0:256], lhsT=sc_[0:NT, 0:NT], rhs=v_[0:NT, :],
                                                                           start=True, stop=False), reads=[sc_.r, v_.r], writes=[O_.r])
                for c in range(NC):
                    if samp:
                        s0, s0b, s1 = S0.next(), S0b.next(), S1.next()
                        k.dma("sp", s0[:, :], T["state_gla"][l, 16 * ((tt - SEQ) // 64) + c, hh, :, :], writes=[s0.r])
                        k.op("pool", lambda h, s0=s0, s0b=s0b: h.tensor_copy(out=s0b[:, :], in_=s0[:, :]), reads=[s0.r], writes=[s0b.r])
                        Sb_ap, Sb_res = s0b[:, :], s0b.r
                    else:
                        Sb_ap, Sb_res = Sbf[:, hh, :], Sbf.r
                    k.op("pe", lambda h, O_=O_, qz3=qz3, c=c, Sb_ap=Sb_ap, NT=NT, NC=NC: h.matmul(
                        out=O_[0:NT, 0:256], lhsT=qz3[:, c, :], rhs=Sb_ap, start=False, stop=(c == NC - 1)),
                        reads=[qz.r, Sb_res], writes=[O_.r])
                    SU = psSU.next()
                    k.op("pe", lambda h, SU=SU, kz3=kz3, c=c, v_=v_, NT=NT: h.matmul(out=SU[:, 0:256], lhsT=kz3[:, c, :], rhs=v_[0:NT, :],
                                                                                     start=True, stop=True), reads=[kz.r, v_.r], writes=[SU.r])
                    if samp:
                        k.op("dve", lambda h, s1=s1, s0=s0, e4=e4, c=c, SU=SU: h.scalar_tensor_tensor(
                            out=s1[:, :], in0=s0[:, :], scalar=e4[:, c:c + 1], in1=SU[:, 0:256], op0=ALU.mult, op1=ALU.add),
                            reads=[s0.r, e4.r, SU.r], writes=[s1.r])
                        k.dma("sp", T["gla_s"][l, 16 * ((tt - SEQ) // 64) + c, hh, :, :], s1[:, :], reads=[s1.r])
                    else:
                        k.op("dve", lambda h, hh=hh, e4=e4, c=c, SU=SU: h.scalar_tensor_tensor(
                            out=S[:, hh, :], in0=S[:, hh, :], scalar=e4[:, c:c + 1], in1=SU[:, 0:256], op0=ALU.mult, op1=ALU.add),
                            reads=[S.r, e4.r, SU.r], writes=[S.r])
                        k.op("act", lambda h, hh=hh: h.copy(out=Sbf[:, hh, :], in_=S[:, hh, :]), reads=[S.r], writes=[Sbf.r])
                s_ = st.next()
                k.op("act", lambda h, O_=O_, s_=s_, NT=NT: h.activation(out=junk[0:NT, :], in_=O_[0:NT, 0:256], func=AF.Square,
                                                                        accum_out=s_[0:NT, 0:1]), reads=[O_.r], writes=[junk.r, s_.r])
                k.op("act", lambda h, s_=s_, NT=NT: h.activation(out=s_[0:NT, 1:2], in_=s_[0:NT, 0:1], func=AF.Ln, scale=1.0 / 256, bias=EPS),
                     reads=[s_.r], writes=[s_.r])
                k.op("act", lambda h, s_=s_, NT=NT: h.activation(out=s_[0:NT, 2:3], in_=s_[0:NT, 1:2], func=AF.Exp, scale=-0.5),
                     reads=[s_.r], writes=[s_.r])
                oc_ = oc.next()
                k.op("dve", lambda h, oc_=oc_, O_=O_, s_=s_, g_=g_, NT=NT: h.scalar_tensor_tensor(
                    out=oc_[0:NT, :], in0=O_[0:NT, 0:256], scalar=s_[0:NT, 2:3], in1=g_[0:NT, :], op0=ALU.mult, op1=ALU.mult),
                    reads=[O_.r, s_.r, g_.r], writes=[oc_.r])
                for ec in range(2):
                    k.op("pe", lambda h, oc_=oc_, ec=ec, NT=NT: h.transpose(out=psOT[:, ec, 0:NT], in_=oc_[0:NT, ec * 128:(ec + 1) * 128],
                                                                           identity=ident[0:NT, 0:NT]), reads=[oc_.r, ident.r], writes=[psOT.r])
                ot = ocT.next()
                k.op("act", lambda h, ot=ot, NT=NT: h.copy(out=ot[:, :, 0:NT], in_=psOT[:, 0:2, 0:NT]), reads=[psOT.r], writes=[ot.r])
                k.dma("sp", T["OCT"].rearrange("(hc p) t -> p hc t", p=128)[:, 2 * hh:2 * hh + 2, tt:tt + NT], ot[:, :, 0:NT], reads=[ot.r])
            if tt == SEQ - 128:
                k.dma("sp", T["gla_p"][l].rearrange("h d e -> d h e"), S[:, :, :], reads=[S.r])
        k.barrier()
        k.emit()


def phase_M(k, l, T):
    with ExitStack() as es:
        wpa = sb(k, es, "wpa", [128, 4, D], BF16)
        wpb = sb(k, es, "wpb", [128, 4, D], BF16)
        wpc = sb(k, es, "wpc", [128, 8, D], BF16)
        wo = sb(k, es, "wo", [128, 8, D], BF16)
        k.dma("pool", wpa[:, :, :], T["w_pa"][l].rearrange("(c p) n -> p c n", p=128), writes=[wpa.r])
        k.dma("pool", wpb[:, :, :], T["w_pb"][l].rearrange("(c p) n -> p c n", p=128), writes=[wpb.r])
        k.dma("pool", wpc[:, :, :], T["w_pc"][l].rearrange("(c p) n -> p c n", p=128), writes=[wpc.r])
        k.dma("pool", wo[:, :, :], T["w_o"][l].rearrange("(c p) n -> p c n", p=128), writes=[wo.r])
        OA = sbring(k, es, "OA", [128, 4, 512], BF16, 2)
        OB = sbring(k, es, "OB", [128, 4, 512], BF16, 2)
        OC = sbring(k, es, "OC", [128, 8, 512], BF16, 2)
        MG = sbring(k, es, "MG", [128, 512], F32, 6)
        acc = sbring(k, es, "accM", [128, 512], F32, 2)
        tmp = sbring(k, es, "tmpM", [128, 512], F32, 3)
        mT = sbring(k, es, "mT", [128, 8, 512], BF16, 2)
        xr = sbring(k, es, "xM", [128, D], F32, 2)
        yr = sbring(k, es, "yM", [128, D], F32, 2)
        pp = psring(k, es, "ppM", [128, 512], F32, 6)
        blocks = [(512 * i, 512) for i in range(16)] + [(SEQ, NST)]
        for (t0, n) in blocks:
            oa_, ob_, oc_ = OA.next(), OB.next(), OC.next()
            k.dma("sp", oa_[:, :, 0:n], T["OAT"].rearrange("(c p) t -> p c t", p=128)[:, :, t0:t0 + n], writes=[oa_.r])
            k.dma("sp", ob_[:, :, 0:n], T["OBT"].rearrange("(c p) t -> p c t", p=128)[:, :, t0:t0 + n], writes=[ob_.r])
            k.dma("sp", oc_[:, :, 0:n], T["OCT"].rearrange("(c p) t -> p c t", p=128)[:, :, t0:t0 + n], writes=[oc_.r])
            m_ = mT.next()
            for dmc in range(8):
                cs = slice(dmc * 128, (dmc + 1) * 128)
                mg = []
                for br in range(3):
                    g_ = MG.next()
                    r0 = br * D + dmc * 128
                    k.dma("sp", g_[:, 0:n], T["MGT"][r0:r0 + 128, t0:t0 + n], writes=[g_.r])
                    mg.append(g_)
                a_ = acc.next()
                for br, (src, w_, nfc) in enumerate(((oa_, wpa, 4), (ob_, wpb, 4), (oc_, wpc, 8))):
                    p_ = pp.next()
                    for fc in range(nfc):
                        k.op("pe", lambda h, p_=p_, w_=w_, fc=fc, cs=cs, src=src, n=n, nfc=nfc: h.matmul(
                            out=p_[:, 0:n], lhsT=w_[:, fc, cs], rhs=src[:, fc, 0:n], start=(fc == 0), stop=(fc == nfc - 1)),
                            reads=[w_.r, src.r], writes=[p_.r])
                    g_ = mg[br]
                    if br == 0:
                        k.op("dve", lambda h, a_=a_, p_=p_, g_=g_, n=n: h.tensor_tensor(out=a_[:, 0:n], in0=p_[:, 0:n], in1=g_[:, 0:n], op=ALU.mult),
                             reads=[p_.r, g_.r], writes=[a_.r])
                    else:
                        t_ = tmp.next()
                        k.op("dve", lambda h, t_=t_, p_=p_, g_=g_, n=n: h.tensor_tensor(out=t_[:, 0:n], in0=p_[:, 0:n], in1=g_[:, 0:n], op=ALU.mult),
                             reads=[p_.r, g_.r], writes=[t_.r])
                        if br == 1:
                            k.op("pool", lambda h, a_=a_, t_=t_, n=n: h.tensor_tensor(out=a_[:, 0:n], in0=a_[:, 0:n], in1=t_[:, 0:n], op=ALU.add),
                                 reads=[a_.r, t_.r], writes=[a_.r])
                        else:
                            k.op("pool", lambda h, a_=a_, t_=t_, m_=m_, dmc=dmc, n=n: h.tensor_tensor(
                                out=m_[:, dmc, 0:n], in0=a_[:, 0:n], in1=t_[:, 0:n], op=ALU.add), reads=[a_.r, t_.r], writes=[m_.r])
            tiles = [(t0 + i * 128, 128) for i in range(n // 128)] if n >= 128 else [(t0, n)]
            for (tt, np_) in tiles:
                c0 = tt - t0
                x_ = xr.next()
                if l == 0:
                    src = T["x_p"][tt:tt + np_, :] if tt < SEQ else T["x_s"][tt - SEQ:tt - SEQ + np_, :]
                else:
                    src = T["Y"][tt:tt + np_, :]
                k.dma("sp", x_[0:np_, :], src, writes=[x_.r])
                y_ = yr.next()
                for hf in range(2):
                    p_ = pp.next()
                    for dmc in range(8):
                        k.op("pe", lambda h, p_=p_, m_=m_, dmc=dmc, c0=c0, np_=np_, hf=hf: h.matmul(
                            out=p_[0:np_, :], lhsT=m_[:, dmc, c0:c0 + np_], rhs=wo[:, dmc, hf * 512:(hf + 1) * 512],
                            start=(dmc == 0), stop=(dmc == 7)), reads=[m_.r, wo.r], writes=[p_.r])
                    k.op("dve", lambda h, y_=y_, p_=p_, x_=x_, np_=np_, hf=hf: h.tensor_tensor(
                        out=y_[0:np_, hf * 512:(hf + 1) * 512], in0=p_[0:np_, :], in1=x_[0:np_, hf * 512:(hf + 1) * 512], op=ALU.add),
                        reads=[p_.r, x_.r], writes=[y_.r])
                if l == DEPTH - 1:
                    dst = T["y_p"][tt:tt + np_, :] if tt < SEQ else T["y_s"][tt - SEQ:tt - SEQ + np_, :]
                else:
                    dst = T["Y"][tt:tt + np_, :]
                k.dma("sp", dst, y_[0:np_, :], reads=[y_.r])
        k.barrier()
        k.emit()


def _consts():
    bf = ml_dtypes.bfloat16
    c = {}
    c["c_ident"] = np.eye(128, dtype=np.float32).astype(bf)
    c["c_identf"] = np.eye(128, dtype=np.float32)
    jj, kk = np.meshgrid(np.arange(128), np.arange(128), indexing="ij")
    c["c_tri"] = (jj >= kk).astype(np.float32).astype(bf)
    c["c_comp"] = (jj < kk).astype(np.float32).astype(bf)
    kq = np.arange(128)[:, None, None] + 128 * np.arange(4)[None, :, None]
    c["c_maskd"] = (kq < np.arange(512)[None, None, :]).astype(np.float32)
    mn = np.zeros((128, 32), np.float32)
    for kx in range(4):
        for col in range(32):
            mn[kx, col] = 1.0 if kx < (col % 4) else 0.0
    c["c_maskn"] = mn
    c["c_iota"] = np.arange(128, dtype=np.float32)[:, None].copy()
    invc = np.zeros((128, 4, 128), np.float32)
    for g, w in enumerate(POOL_W):
        invc[:, g, :] = 1.0 / np.minimum(w, np.arange(128) + 1)[None, :]
    c["c_invc"] = invc
    for tag, NT, C in (("p", 128, 64), ("s", 64, 4)):
        NC = NT // C
        s_, t_ = np.meshgrid(np.arange(NT), np.arange(NT), indexing="ij")
        c["c_sm_" + tag] = ((s_ // C == t_ // C) & (s_ <= t_)).astype(np.float32)
        ch = np.zeros((128, NC, NT), np.float32)
        rm = np.zeros((NT, NC), np.float32)
        for cc in range(NC):
            ch[:, cc, cc * C:(cc + 1) * C] = 1.0
            rm[cc * C:(cc + 1) * C, cc] = 1.0
        c["c_ch_" + tag] = ch
        c["c_rm_" + tag] = rm
        rs = np.ones((128, NT), np.float32)
        rs[:, ::C] = 0.0
        c["c_rs_" + tag] = rs
    return c


_IN_SHAPES = {
    "x_p": ([SEQ, D], F32), "x_s": ([NST, D], F32),
    "cache_k": ([DEPTH, NPOOL * 128, 512], F32), "cache_v": ([DEPTH, NPOOL * 128, 512], F32),
    "state_pool": ([DEPTH, NS, 15, 512], F32), "state_gla": ([DEPTH, NS, 4, 128, 256], F32),
    "page_table": ([NS, NPAGE], I32), "norm_g": ([DEPTH, D], F32), "w_in": ([DEPTH, D, N_IN], F32),
    "sb_qnorm_g": ([DEPTH, 64], F32), "sb_knorm_g": ([DEPTH, 64], F32), "sb_bias": ([DEPTH, 8], F32),
    "pool_w": ([DEPTH, 4, 128, 128], F32), "pool_scale": ([DEPTH, 512], F32), "gla_w2": ([DEPTH, 16, 512], F32),
    "gla_b2": ([DEPTH, 512], F32), "gla_onorm_g": ([DEPTH, 256], F32), "w_pa": ([DEPTH, 512, D], F32),
    "w_pb": ([DEPTH, 512, D], F32), "w_pc": ([DEPTH, D, D], F32), "w_o": ([DEPTH, D, D], F32),
}
_OUT_SHAPES = {
    "y_p": [SEQ, D], "y_s": [NST, D], "k_p": [DEPTH, SEQ, 512], "v_p": [DEPTH, SEQ, 512],
    "pool_p": [DEPTH, 15, 512], "gla_p": [DEPTH, 4, 128, 256], "k_s": [DEPTH, NST, 512], "v_s": [DEPTH, NST, 512],
    "pool_s": [DEPTH, NS, 15, 512], "gla_s": [DEPTH, NS, 4, 128, 256],
}
_SCRATCH = {
    "QT": ([512, NSLOT], BF16), "KT": ([512, NSLOT], BF16), "Vb": ([NSLOT, 512], BF16),
    "SGT": ([512, NSLOT], F32), "UT": ([512, NSLOT], F32), "PGT": ([512, NSLOT], F32),
    "GQT": ([512, NSLOT], F32), "GKT": ([512, NSLOT], F32), "LAT": ([512, NSLOT], F32),
    "GV": ([NSLOT, D], BF16), "GG": ([NSLOT, D], F32), "MGT": ([3 * D, NSLOT], F32),
    "OAT": ([512, NSLOT], BF16), "OBT": ([512, NSLOT], BF16), "OCT": ([D, NSLOT], BF16),
    "Y": ([NSLOT, D], F32),
}


def build_nc(phases=("P", "A", "B", "C", "M"), layers=(0, 1), plan=None):
    nc = bass.Bass("TRN2", target_bir_lowering=False)
    T = {}
    consts = _consts()
    for name, (shape, dt) in _IN_SHAPES.items():
        T[name] = nc.dram_tensor(name, shape, dt, kind="ExternalInput").ap()
    for name, arr in consts.items():
        dt = BF16 if arr.dtype == ml_dtypes.bfloat16 else F32
        T[name] = nc.dram_tensor(name, list(arr.shape), dt, kind="ExternalInput").ap()
    for name, shape in _OUT_SHAPES.items():
        T[name] = nc.dram_tensor(name, shape, F32, kind="ExternalOutput").ap()
    for name, (shape, dt) in _SCRATCH.items():
        T[name] = nc.dram_tensor(name, shape, dt, kind="Internal").ap()
    fns = {"P": phase_P, "A": phase_A, "B": phase_B, "C": phase_C, "M": phase_M}
    with ExitStack() as es:
        k = K(nc, es)
        if plan is None:
            plan = [(ph, l) for l in layers for ph in phases]
        for ph, l in plan:
            fns[ph](k, l, T)
    return nc, consts


def kernel(x_prompt, x_sample, cache_k, cache_v, state_pool, state_gla, page_table, norm_g, w_in,
           sb_qnorm_g, sb_knorm_g, sb_bias, pool_w, pool_scale, gla_w2, gla_b2, gla_onorm_g, w_pa, w_pb, w_pc, w_o):
    ncores = NCORES
    nc, consts = build_nc()
    f = lambda a: np.ascontiguousarray(np.asarray(a, dtype=np.float32))
    ck = f(cache_k).reshape(DEPTH, NPOOL * 128, 512)
    cv = f(cache_v).reshape(DEPTH, NPOOL * 128, 512)
    shared = {
        "cache_k": ck, "cache_v": cv, "norm_g": f(norm_g), "w_in": f(w_in), "sb_qnorm_g": f(sb_qnorm_g),
        "sb_knorm_g": f(sb_knorm_g), "sb_bias": f(sb_bias), "pool_w": f(pool_w), "pool_scale": f(pool_scale),
        "gla_w2": f(gla_w2), "gla_b2": f(gla_b2), "gla_onorm_g": f(gla_onorm_g), "w_pa": f(w_pa), "w_pb": f(w_pb),
        "w_pc": f(w_pc), "w_o": f(w_o),
    }
    shared.update(consts)
    xp = f(x_prompt)
    xs = f(x_sample)
    sp = f(state_pool)
    sg = f(state_gla)
    pt = np.ascontiguousarray(np.asarray(page_table, dtype=np.int32))
    in_maps = []
    for c in range(ncores):
        m = dict(shared)
        m["x_p"] = xp[c]
        m["x_s"] = np.ascontiguousarray(xs[NS * c:NS * (c + 1)].reshape(NST, D))
        m["state_pool"] = np.ascontiguousarray(sp[:, NS * c:NS * (c + 1)])
        m["state_gla"] = np.ascontiguousarray(sg[:, NS * c:NS * (c + 1)])
        m["page_table"] = np.ascontiguousarray(pt[NS * c:NS * (c + 1)])
        in_maps.append(m)
    res = run_bass_kernel_spmd(nc, in_maps, core_ids=list(range(ncores)))
    R = res.results
    B = 2
    y_prompt = np.stack([R[b]["y_p"] for b in range(B)]).astype(np.float32)
    y_sample = np.concatenate([R[c]["y_s"].reshape(NS, 4, D) for c in range(ncores)], axis=0).astype(np.float32)
    k_prompt = np.stack([R[b]["k_p"] for b in range(B)], axis=1).reshape(DEPTH, B, SEQ, 8, 64).astype(np.float32)
    v_prompt = np.stack([R[b]["v_p"] for b in range(B)], axis=1).reshape(DEPTH, B, SEQ, 8, 64).astype(np.float32)
    pool_prompt = np.stack([R[b]["pool_p"] for b in range(B)], axis=1).astype(np.float32)
    gla_prompt = np.stack([R[b]["gla_p"] for b in range(B)], axis=1).astype(np.float32)
    k_sample = np.concatenate([R[c]["k_s"].reshape(DEPTH, NS, 4, 8, 64) for c in range(ncores)], axis=1).astype(np.float32)
    v_sample = np.concatenate([R[c]["v_s"].reshape(DEPTH, NS, 4, 8, 64) for c in range(ncores)], axis=1).astype(np.float32)
    pool_sample = np.concatenate([R[c]["pool_s"] for c in range(ncores)], axis=1).astype(np.float32)
    gla_sample = np.concatenate([R[c]["gla_s"] for c in range(ncores)], axis=1).astype(np.float32)
    return (y_prompt, y_sample, k_prompt, v_prompt, pool_prompt, gla_prompt, k_sample, v_sample, pool_sample, gla_sample)
```

```python
from contextlib import ExitStack
import numpy as np
import ml_dtypes
import concourse.bass as bass
import concourse.mybir as mybir
from concourse.bass_utils import run_bass_kernel_spmd

F32 = mybir.dt.float32
BF16 = mybir.dt.bfloat16
I32 = mybir.dt.int32
AF = mybir.ActivationFunctionType
ALU = mybir.AluOpType
AX = mybir.AxisListType

D = 1024
SEQ = 8192
NCORES = 2
NSG = 4
NS = 16 * NSG
NST = 4 * NS
NTOK = SEQ + NST
NSLOT = SEQ + NST
DEPTH = 2
NPAGE = 16
NPOOL = 2560
N_IN = 9232
EPS = 1e-6
POOL_W = (2, 4, 8, 16)
K_LIMIT = 10 ** 12


class Res:
    __slots__ = ("name", "w", "r")

    def __init__(self, name=""):
        self.name = name
        self.w = None
        self.r = {}


class Eng:
    def __init__(self, name, sem, is_pe=False):
        self.name = name
        self.sem = sem
        self.count = 0
        self.known = {}
        self.ops = []
        self.is_pe = is_pe
        self.slots = []
        self.slot_i = 0


class K:
    def __init__(self, nc, es):
        self.nc = nc
        self.es = es
        self.sem_objs = {}
        names = ["pe", "act", "dve", "pool", "sp"]
        self.eng = {}
        for n in names:
            s = es.enter_context(nc.semaphore("sem_" + n))
            self.eng[n] = Eng(n, s, is_pe=(n == "pe"))
        for q, nslot in (("sp", 12), ("pool", 4), ("act", 4)):
            for i in range(nslot):
                s = es.enter_context(nc.semaphore(f"dq_{q}_{i}"))
                self.eng[q].slots.append([s, 0])
        self.bar = es.enter_context(nc.semaphore("bar"))
        self.bar_n = 0
        self.all_res = []

    def res(self, name=""):
        r = Res(name)
        self.all_res.append(r)
        return r

    def _collect(self, e, reads, writes):
        waits = {}

        def need(ev):
            sem, val = ev
            if e.is_pe and sem is e.sem:
                return
            if e.known.get(id(sem), 0) >= val:
                return
            k = id(sem)
            if k not in waits or waits[k][1] < val:
                waits[k] = (sem, val)

        for r in reads:
            if r.w is not None:
                need(r.w)
        for w in writes:
            if w.w is not None:
                need(w.w)
            for ev in w.r.values():
                need(ev)
        return waits, need

    def _commit(self, e, waits, ev, reads, writes):
        for k, (sem, val) in waits.items():
            e.known[k] = val
        for r in reads:
            r.r[id(ev[0])] = ev
        for w in writes:
            w.w = ev
            w.r = {}

    def _lim(self):
        self.nrec = getattr(self, "nrec", 0) + 1
        return self.nrec > K_LIMIT

    def op(self, en, fn, reads=(), writes=()):
        if self._lim():
            return
        e = self.eng[en]
        waits, _ = self._collect(e, reads, writes)
        e.count += 1
        ev = (e.sem, e.count)
        self._commit(e, waits, ev, reads, writes)
        e.ops.append((list(waits.values()), fn, (e.sem, 1)))

    def dma(self, q, out, in_, reads=(), writes=(), **kw):
        if self._lim():
            return
        e = self.eng[q]
        slot = e.slots[e.slot_i]
        e.slot_i = (e.slot_i + 1) % len(e.slots)
        waits, need = self._collect(e, reads, writes)
        if slot[1] > 0:
            need((slot[0], slot[1] * 16))
        slot[1] += 1
        ev = (slot[0], slot[1] * 16)
        self._commit(e, waits, ev, reads, writes)
        e.ops.append((list(waits.values()), (lambda h, out=out, in_=in_, kw=kw: h.dma_start(out=out, in_=in_, **kw)),
                      (slot[0], 16)))

    def raw_dma(self, q, fn, reads=(), writes=()):
        if self._lim():
            return
        e = self.eng[q]
        slot = e.slots[e.slot_i]
        e.slot_i = (e.slot_i + 1) % len(e.slots)
        waits, need = self._collect(e, reads, writes)
        if slot[1] > 0:
            need((slot[0], slot[1] * 16))
        slot[1] += 1
        ev = (slot[0], slot[1] * 16)
        self._commit(e, waits, ev, reads, writes)
        e.ops.append((list(waits.values()), fn, (slot[0], 16)))

    def barrier(self):
        self.bar_n += 1
        target = self.bar_n * 5
        for n, e in self.eng.items():
            waits = []
            for s, uses in e.slots:
                if uses > 0 and e.known.get(id(s), 0) < uses * 16:
                    waits.append((s, uses * 16))
                    e.known[id(s)] = uses * 16
            if e.count > 0:
                waits.append((e.sem, e.count))
            e.ops.append((waits, "bar_inc", None))
            e.ops.append(([(self.bar, target)], None, None))
        for r in self.all_res:
            r.w = None
            r.r = {}

    def emit(self):
        nc = self.nc
        handles = {"pe": "tensor", "act": "scalar", "dve": "vector", "pool": "gpsimd", "sp": "sync"}
        with nc.Block() as block:
            for n, e in self.eng.items():
                ops = e.ops
                e.ops = []
                bar = self.bar

                def body(h, ops=ops, bar=bar):
                    for waits, fn, inc in ops:
                        for sem, val in waits:
                            h.wait_ge(sem, val)
                        if fn is None:
                            continue
                        if fn == "bar_inc":
                            h.sem_inc(bar, 1)
                            continue
                        ins = fn(h)
                        ins.then_inc(inc[0], inc[1])

                getattr(block, handles[n])(body)


class Buf:
    def __init__(self, k, t, name):
        self.t = t
        self.r = k.res(name)

    def __getitem__(self, idx):
        return self.t[idx]


class Ring:
    def __init__(self, bufs):
        self.bufs = bufs
        self.i = 0

    def next(self):
        b = self.bufs[self.i]
        self.i = (self.i + 1) % len(self.bufs)
        return b


_uid = [0]


def _uname(name):
    _uid[0] += 1
    return f"{name}_{_uid[0]}"


def sb(k, es, name, shape, dt):
    return Buf(k, es.enter_context(k.nc.sbuf_tensor(_uname(name), shape, dt)), name)


def ps(k, es, name, shape, dt):
    return Buf(k, es.enter_context(k.nc.psum_tensor(_uname(name), shape, dt)), name)


def sbring(k, es, name, shape, dt, n):
    return Ring([sb(k, es, f"{name}{i}", shape, dt) for i in range(n)])


def psring(k, es, name, shape, dt, n):
    return Ring([ps(k, es, f"{name}{i}", shape, dt) for i in range(n)])


def tile_list():
    return [(i * 128, 128) for i in range(64)] + [(SEQ + 64 * g, 64) for g in range(NSG)]


def phase_P(k, l, T):
    nc = k.nc
    with ExitStack() as es:
        ng_b = sb(k, es, "ng_b", [128, D], F32)
        qg_b = sb(k, es, "qg_b", [128, 64], F32)
        kg_b = sb(k, es, "kg_b", [128, 64], F32)
        og_b = sb(k, es, "og_b", [128, 256], F32)
        w2 = sb(k, es, "w2", [16, 512], BF16)
        nb2 = sb(k, es, "nb2", [128, 4], F32)
        ident = sb(k, es, "identP", [128, 128], BF16)
        hT = sb(k, es, "hT", [128, 8, 1024], BF16)
        Wring = sbring(k, es, "Wt", [128, 8, 512], BF16, 2)
        Wr = sb(k, es, "Wr", [128, 8, 16], BF16)
        xring = sbring(k, es, "xt", [128, D], F32, 2)
        junk = sb(k, es, "junkP", [128, D], BF16)
        hb = sbring(k, es, "hb", [128, D], BF16, 2)
        st = sbring(k, es, "stP", [128, 16], F32, 4)
        sq = sbring(k, es, "sqP", [128, 512], F32, 2)
        f32o = sbring(k, es, "f32o", [128, 512], F32, 4)
        bfo = sbring(k, es, "bfo", [128, 512], BF16, 4)
        rT = sb(k, es, "rT", [16, 1024], BF16)
        psT = ps(k, es, "psT", [128, 8, 128], BF16)
        psM = psring(k, es, "psM", [128, 512], F32, 4)
        psQ = psring(k, es, "psQ", [128, 8, 128], BF16, 2)

        k.dma("sp", ng_b[:, :], T["norm_g"][l:l + 1, :].to_broadcast([128, D]), writes=[ng_b.r])
        k.dma("sp", qg_b[:, :], T["sb_qnorm_g"][l:l + 1, :].to_broadcast([128, 64]), writes=[qg_b.r])
        k.dma("sp", kg_b[:, :], T["sb_knorm_g"][l:l + 1, :].to_broadcast([128, 64]), writes=[kg_b.r])
        k.dma("sp", og_b[:, :], T["gla_onorm_g"][l:l + 1, :].to_broadcast([128, 256]), writes=[og_b.r])
        k.dma("pool", w2[:, :], T["gla_w2"][l], writes=[w2.r])
        k.dma("sp", nb2[:, :], T["gla_b2"][l].rearrange("(c p) -> p c", p=128), writes=[nb2.r],
              allow_slow_non_contiguous=True)
        k.dma("sp", ident[:, :], T["c_ident"][:, :], writes=[ident.r])
        k.op("dve", lambda h: h.tensor_scalar(out=qg_b[:, :], in0=qg_b[:, :], scalar1=0.125, scalar2=None,
                                              op0=ALU.mult), reads=[qg_b.r], writes=[qg_b.r])
        k.op("dve", lambda h: h.tensor_scalar(out=nb2[:, :], in0=nb2[:, :], scalar1=-1.0, scalar2=None,
                                              op0=ALU.mult), reads=[nb2.r], writes=[nb2.r])

        w_in = T["w_in"][l].rearrange("(kc p) n -> p kc n", p=128)
        sblocks = [(i * 1024, 1024) for i in range(8)] + [(SEQ, NST)]
        evac_i = [0]

        def evac_engine():
            evac_i[0] += 1
            return "act" if evac_i[0] % 2 else "dve"

        for (t0, ntok) in sblocks:
            if t0 < SEQ:
                tiles = [(t0 + i * 128, 128) for i in range(ntok // 128)]
                halves = [(0, 512), (512, 512)]
            else:
                tiles = [(SEQ + 64 * g, 64) for g in range(NSG)]
                halves = [(0, NST)]
            for (tt, np_) in tiles:
                c0 = tt - t0
                xt = xringe = xring.next()
                src = T["x_p"][tt:tt + np_, :] if tt < SEQ else T["x_s"][tt - SEQ:tt - SEQ + np_, :]
                if l > 0:
                    src = T["Y"][tt:tt + np_, :]
                k.dma("sp", xt[0:np_, :], src, writes=[xt.r])
                s = st.next()
                k.op("act", lambda h, xt=xt, s=s, np_=np_: h.activation(out=junk[0:np_, :], in_=xt[0:np_, :], func=AF.Square,
                                                                         accum_out=s[0:np_, 0:1]),
                     reads=[xt.r], writes=[junk.r, s.r])
                k.op("act", lambda h, s=s, np_=np_: h.activation(out=s[0:np_, 1:2], in_=s[0:np_, 0:1], func=AF.Ln,
                                                                 scale=1.0 / D, bias=EPS), reads=[s.r], writes=[s.r])
                k.op("act", lambda h, s=s, np_=np_: h.activation(out=s[0:np_, 2:3], in_=s[0:np_, 1:2], func=AF.Exp,
                                                                 scale=-0.5), reads=[s.r], writes=[s.r])
                hbt = hb.next()
                k.op("dve", lambda h, xt=xt, s=s, hbt=hbt, np_=np_: h.scalar_tensor_tensor(
                    out=hbt[0:np_, :], in0=xt[0:np_, :], scalar=s[0:np_, 2:3], in1=ng_b[0:np_, :], op0=ALU.mult, op1=ALU.mult),
                    reads=[xt.r, s.r, ng_b.r], writes=[hbt.r])
                for kc in range(8):
                    k.op("pe", lambda h, hbt=hbt, kc=kc, np_=np_: h.transpose(out=psT[:, kc, 0:np_], in_=hbt[0:np_, kc * 128:(kc + 1) * 128],
                                                                               identity=ident[0:np_, 0:np_]),
                         reads=[hbt.r, ident.r], writes=[psT.r])
                k.op("act", lambda h, c0=c0, np_=np_: h.copy(out=hT[:, :, c0:c0 + np_], in_=psT[:, :, 0:np_]),
                     reads=[psT.r], writes=[hT.r])

            def load_w(c0):
                Wt = Wring.next()
                k.dma("pool", Wt[:, :, :], w_in[:, :, c0:c0 + 512], writes=[Wt.r])
                return Wt

            def tokmajor(Wt, tt, np_):
                c0 = tt - t0
                p = psM.next()
                for kc in range(8):
                    k.op("pe", lambda h, p=p, kc=kc, c0=c0, np_=np_, Wt=Wt: h.matmul(
                        out=p[0:np_, :], lhsT=hT[:, kc, c0:c0 + np_], rhs=Wt[:, kc, :], start=(kc == 0), stop=(kc == 7)),
                        reads=[hT.r, Wt.r], writes=[p.r])
                return p

            def featmajor(Wt, cc, h0, n, ncols=128, Wsl=None):
                p = psM.next()
                for kc in range(8):
                    lhs = Wt[:, kc, cc * 128:cc * 128 + ncols] if Wsl is None else Wsl(kc)
                    k.op("pe", lambda h, p=p, kc=kc, lhs=lhs, h0=h0, n=n, ncols=ncols: h.matmul(
                        out=p[0:ncols, 0:n], lhsT=lhs, rhs=hT[:, kc, h0:h0 + n], start=(kc == 0), stop=(kc == 7)),
                        reads=[hT.r, Wt.r], writes=[p.r])
                return p

            def qknorm(p, np_, gb, out_f32, out_bf):
                s_ = sq.next()
                k.op("act", lambda h: h.activation(out=s_[0:np_, :], in_=p[0:np_, :], func=AF.Square),
                     reads=[p.r], writes=[s_.r])
                s = st.next()
                k.op("dve", lambda h: h.tensor_reduce(out=s[0:np_, 0:8], in_=s_[0:np_, :].rearrange("p (a b) -> p a b", b=64),
                                                      axis=AX.X, op=ALU.add), reads=[s_.r], writes=[s.r])
                k.op("act", lambda h: h.activation(out=s[0:np_, 8:16], in_=s[0:np_, 0:8], func=AF.Ln, scale=1.0 / 64, bias=EPS),
                     reads=[s.r], writes=[s.r])
                k.op("act", lambda h: h.activation(out=s[0:np_, 0:8], in_=s[0:np_, 8:16], func=AF.Exp, scale=-0.5),
                     reads=[s.r], writes=[s.r])
                k.op("dve", lambda h: h.tensor_tensor(out=s_[0:np_, :].rearrange("p (a b) -> p a b", b=64),
                                                      in0=p[0:np_, :].rearrange("p (a b) -> p a b", b=64),
                                                      in1=s[0:np_, 0:8].unsqueeze(2).to_broadcast([np_, 8, 64]), op=ALU.mult),
                     reads=[p.r, s.r], writes=[s_.r])
                gbb = gb[0:np_, :].unsqueeze(1).to_broadcast([np_, 8, 64])
                if out_f32 is not None:
                    k.op("dve", lambda h: h.tensor_tensor(out=out_f32[0:np_, :].rearrange("p (a b) -> p a b", b=64),
                                                          in0=s_[0:np_, :].rearrange("p (a b) -> p a b", b=64), in1=gbb, op=ALU.mult),
                         reads=[s_.r, gb.r], writes=[out_f32.r])
                    k.op("pool", lambda h: h.tensor_copy(out=out_bf[0:np_, :], in_=out_f32[0:np_, :]),
                         reads=[out_f32.r], writes=[out_bf.r])
                else:
                    k.op("dve", lambda h: h.tensor_tensor(out=out_bf[0:np_, :].rearrange("p (a b) -> p a b", b=64),
                                                          in0=s_[0:np_, :].rearrange("p (a b) -> p a b", b=64), in1=gbb, op=ALU.mult),
                         reads=[s_.r, gb.r], writes=[out_bf.r])

            def to_featmajor_scratch(src_bf, np_, dst, tt):
                pq = psQ.next()
                for hp in range(4):
                    k.op("pe", lambda h, hp=hp: h.transpose(out=pq[:, hp, 0:np_], in_=src_bf[0:np_, hp * 128:(hp + 1) * 128],
                                                            identity=ident[0:np_, 0:np_]),
                         reads=[src_bf.r, ident.r], writes=[pq.r])
                o = bfo.next()
                k.op("act", lambda h: h.copy(out=o[:, 0:4 * np_].rearrange("p (a b) -> p a b", b=np_), in_=pq[:, 0:4, 0:np_]),
                     reads=[pq.r], writes=[o.r])
                k.dma("sp", dst.rearrange("(hp p) t -> p hp t", p=128)[:, :, tt:tt + np_],
                      o[:, 0:4 * np_].rearrange("p (a b) -> p a b", b=np_), reads=[o.r])

            def kv_out(name_p, name_s, tt, np_, srcbuf):
                if tt < SEQ:
                    k.dma("sp", T[name_p][l, tt:tt + np_, :], srcbuf[0:np_, :], reads=[srcbuf.r])
                else:
                    k.dma("sp", T[name_s][l, tt - SEQ:tt - SEQ + np_, :], srcbuf[0:np_, :], reads=[srcbuf.r])

            Wt = load_w(0)
            for (tt, np_) in tiles:
                p = tokmajor(Wt, tt, np_)
                ob = bfo.next()
                qknorm(p, np_, qg_b, None, ob)
                to_featmajor_scratch(ob, np_, T["QT"], tt)
            Wt = load_w(512)
            for (tt, np_) in tiles:
                p = tokmajor(Wt, tt, np_)
                of = f32o.next()
                ob = bfo.next()
                qknorm(p, np_, kg_b, of, ob)
                kv_out("k_p", "k_s", tt, np_, of)
                to_featmajor_scratch(ob, np_, T["KT"], tt)
            Wt = load_w(1024)
            for (tt, np_) in tiles:
                p = tokmajor(Wt, tt, np_)
                of = f32o.next()
                ob = bfo.next()
                k.op("act", lambda h, p=p, of=of, np_=np_: h.copy(out=of[0:np_, :], in_=p[0:np_, :]), reads=[p.r], writes=[of.r])
                k.op("dve", lambda h, of=of, ob=ob, np_=np_: h.tensor_copy(out=ob[0:np_, :], in_=of[0:np_, :]), reads=[of.r], writes=[ob.r])
                kv_out("v_p", "v_s", tt, np_, of)
                k.dma("sp", T["Vb"][tt:tt + np_, :], ob[0:np_, :], reads=[ob.r])

            def feat_group(c0, dst, func, scale=1.0):
                Wt = load_w(c0)
                for cc in range(4):
                    for (h0, n) in halves:
                        p = featmajor(Wt, cc, h0, n)
                        of = f32o.next()
                        if func is None:
                            eng = evac_engine()
                            if eng == "act":
                                k.op("act", lambda h, p=p, of=of, n=n: h.activation(out=of[:, 0:n], in_=p[:, 0:n], func=AF.Copy, scale=scale),
                                     reads=[p.r], writes=[of.r])
                            else:
                                k.op("dve", lambda h, p=p, of=of, n=n: h.tensor_scalar(out=of[:, 0:n], in0=p[:, 0:n], scalar1=scale, scalar2=None,
                                                                                       op0=ALU.mult), reads=[p.r], writes=[of.r])
                        else:
                            k.op("act", lambda h, p=p, of=of, n=n: h.activation(out=of[:, 0:n], in_=p[:, 0:n], func=func),
                                 reads=[p.r], writes=[of.r])
                        r0 = cc * 128
                        k.dma("sp", dst[r0:r0 + 128, t0 + h0:t0 + h0 + n], of[:, 0:n], reads=[of.r])
                return Wt

            feat_group(1536, T["SGT"], AF.Silu)
            Wt_u = feat_group(2048, T["UT"], None)
            if t0 == 7 * 1024:
                p = tokmajor(Wt_u, SEQ - 128, 128)
                of = f32o.next()
                k.op("act", lambda h, p=p, of=of: h.copy(out=of[:, :], in_=p[:, :]), reads=[p.r], writes=[of.r])
                k.dma("sp", T["pool_p"][l, :, :], of[113:128, :], reads=[of.r])
            if t0 == SEQ:
                for g in range(NSG):
                    p = tokmajor(Wt_u, SEQ + 64 * g, 64)
                    of = f32o.next()
                    k.op("act", lambda h, p=p, of=of: h.copy(out=of[0:64, :], in_=p[0:64, :]), reads=[p.r], writes=[of.r])
                    for b in range(16):
                        sq_ = 16 * g + b
                        k.dma("sp", T["pool_s"][l, sq_, 11:15, :], of[4 * b:4 * b + 4, :], reads=[of.r])
                        k.dma("sp", T["pool_s"][l, sq_, 0:11, :], T["state_pool"][l, sq_, 4:15, :])
            feat_group(2560, T["PGT"], AF.Silu)
            feat_group(3072, T["GQT"], None, scale=128.0 ** -0.5)
            feat_group(3584, T["GKT"], None)
            for blk in range(2):
                Wt = load_w(4096 + 512 * blk)
                for (tt, np_) in tiles:
                    p = tokmajor(Wt, tt, np_)
                    ob = bfo.next()
                    k.op(evac_engine() if False else "dve", lambda h, p=p, ob=ob, np_=np_: h.tensor_copy(out=ob[0:np_, :], in_=p[0:np_, :]),
                         reads=[p.r], writes=[ob.r])
                    k.dma("sp", T["GV"][tt:tt + np_, 512 * blk:512 * blk + 512], ob[0:np_, :], reads=[ob.r])
            for blk in range(2):
                Wt = load_w(5120 + 512 * blk)
                for (tt, np_) in tiles:
                    p = tokmajor(Wt, tt, np_)
                    of = f32o.next()
                    k.op("act", lambda h, p=p, of=of, np_=np_: h.activation(out=of[0:np_, :], in_=p[0:np_, :], func=AF.Silu),
                         reads=[p.r], writes=[of.r])
                    k.op("pool", lambda h, of=of, np_=np_: h.tensor_tensor(
                        out=of[0:np_, :].rearrange("p (a b) -> p a b", b=256), in0=of[0:np_, :].rearrange("p (a b) -> p a b", b=256),
                        in1=og_b[0:np_, :].unsqueeze(1).to_broadcast([np_, 2, 256]), op=ALU.mult),
                        reads=[of.r, og_b.r], writes=[of.r])
                    k.dma("sp", T["GG"][tt:tt + np_, 512 * blk:512 * blk + 512], of[0:np_, :], reads=[of.r])
            k.dma("pool", Wr[:, :, :], w_in[:, :, 6144:6160], writes=[Wr.r])
            for (h0, n) in halves:
                p = featmajor(Wr, 0, h0, n, ncols=16, Wsl=lambda kc: Wr[:, kc, :])
                k.op("act", lambda h, p=p, h0=h0, n=n: h.copy(out=rT[0:16, h0:h0 + n], in_=p[0:16, 0:n]), reads=[p.r], writes=[rT.r])
            for cc in range(4):
                for (h0, n) in halves:
                    p = psM.next()
                    k.op("pe", lambda h, p=p, cc=cc, h0=h0, n=n: h.matmul(out=p[:, 0:n], lhsT=w2[0:16, cc * 128:(cc + 1) * 128],
                                                                           rhs=rT[0:16, h0:h0 + n], start=True, stop=True),
                         reads=[w2.r, rT.r], writes=[p.r])
                    of = f32o.next()
                    k.op("act", lambda h, p=p, of=of, cc=cc, n=n: h.activation(out=of[:, 0:n], in_=p[:, 0:n], func=AF.Exp, scale=-1.0,
                                                                                bias=nb2[:, cc:cc + 1]), reads=[p.r, nb2.r], writes=[of.r])
                    k.op("act", lambda h, of=of, n=n: h.activation(out=of[:, 0:n], in_=of[:, 0:n], func=AF.Ln, scale=1.0, bias=1.0),
                         reads=[of.r], writes=[of.r])
                    k.op("dve", lambda h, of=of, n=n: h.tensor_scalar(out=of[:, 0:n], in0=of[:, 0:n], scalar1=-1.0 / 16.0, scalar2=None,
                                                                      op0=ALU.mult), reads=[of.r], writes=[of.r])
                    k.dma("sp", T["LAT"][cc * 128:(cc + 1) * 128, t0 + h0:t0 + h0 + n], of[:, 0:n], reads=[of.r])
            for blk in range(6):
                Wt = load_w(6160 + 512 * blk)
                for cc in range(4):
                    for (h0, n) in halves:
                        p = featmajor(Wt, cc, h0, n)
                        of = f32o.next()
                        k.op("act", lambda h, p=p, of=of, n=n: h.activation(out=of[:, 0:n], in_=p[:, 0:n], func=AF.Sigmoid),
                             reads=[p.r], writes=[of.r])
                        r0 = blk * 512 + cc * 128
                        k.dma("sp", T["MGT"][r0:r0 + 128, t0 + h0:t0 + h0 + n], of[:, 0:n], reads=[of.r])
        k.barrier()
        k.emit()


def phase_A(k, l, T):
    with ExitStack() as es:
        KTp = sb(k, es, "KTp", [128, SEQ], BF16)
        Vp = sb(k, es, "Vp", [128, 64, 128], BF16)
        QTq = sbring(k, es, "QTq", [128, 512], BF16, 2)
        SGq = sbring(k, es, "SGq", [128, 512], F32, 2)
        bias = sb(k, es, "biasA", [128, 8], F32)
        tri = sb(k, es, "triA", [128, 128], BF16)
        comp = sb(k, es, "compA", [128, 128], BF16)
        maskd = sb(k, es, "maskd", [128, 4, 512], F32)
        Er = sbring(k, es, "E", [128, 512], F32, 6)
        SPr = sbring(k, es, "SP", [128, 512], BF16, 6)
        Wr_ = sbring(k, es, "Wa", [128, 512], F32, 4)
        Ar = sbring(k, es, "Aa", [128, 512], BF16, 4)
        oa = sbring(k, es, "oa", [128, 512], BF16, 2)
        Z = psring(k, es, "Z", [128, 512], F32, 4)
        ACC = [ps(k, es, f"ACC{e}", [128, 512], F32) for e in range(2)]
        O = ps(k, es, "O", [128, 512], F32)

        k.dma("sp", bias[:, :], T["sb_bias"][l:l + 1, :].to_broadcast([128, 8]), writes=[bias.r])
        k.dma("sp", tri[:, :], T["c_tri"][:, :], writes=[tri.r])
        k.dma("sp", comp[:, :], T["c_comp"][:, :], writes=[comp.r])
        k.dma("sp", maskd[:, :, :], T["c_maskd"][:, :, :], writes=[maskd.r])

        def block_step(e, Zt, ncol, bias_ap, mask_ap, acc, Vl, Oout, first, a_dt_ring=Ar):
            Et = Er.next()
            k.op("act", lambda h: h.activation(out=Et[:, 0:ncol], in_=Zt[:, 0:ncol], func=AF.Exp, bias=bias_ap, scale=1.0),
                 reads=[Zt.r, bias.r], writes=[Et.r])
            if mask_ap is not None:
                k.op("pool", lambda h: h.tensor_tensor(out=Et[:, 0:ncol], in0=Et[:, 0:ncol], in1=mask_ap, op=ALU.mult),
                     reads=[Et.r, maskd.r], writes=[Et.r])
            SPt = SPr.next()
            k.op("act", lambda h: h.activation(out=SPt[:, 0:ncol], in_=Et[:, 0:ncol], func=AF.Ln, bias=1.0, scale=1.0),
                 reads=[Et.r], writes=[SPt.r])
            if first:
                k.op("dve", lambda h: h.memset(acc[:, 0:ncol], 0.0), writes=[acc.r])
            k.op("pe", lambda h: h.matmul(out=acc[:, 0:ncol], lhsT=tri[:, :], rhs=SPt[:, 0:ncol], start=False, stop=True,
                                          skip_group_check=True), reads=[tri.r, SPt.r, acc.r], writes=[acc.r])
            Wt = Wr_.next()
            k.op("act", lambda h: h.activation(out=Wt[:, 0:ncol], in_=acc[:, 0:ncol], func=AF.Exp, scale=-1.0),
                 reads=[acc.r], writes=[Wt.r])
            k.op("pe", lambda h: h.matmul(out=acc[:, 0:ncol], lhsT=comp[:, :], rhs=SPt[:, 0:ncol], start=False, stop=True,
                                          skip_group_check=True), reads=[comp.r, SPt.r, acc.r], writes=[acc.r])
            At = Ar.next()
            k.op("dve", lambda h: h.tensor_tensor(out=At[:, 0:ncol], in0=Et[:, 0:ncol], in1=Wt[:, 0:ncol], op=ALU.mult),
                 reads=[Et.r, Wt.r], writes=[At.r])
            return At

        for hp in range(4):
            k.dma("sp", KTp[:, :], T["KT"][hp * 128:(hp + 1) * 128, 0:SEQ], writes=[KTp.r])
            k.dma("sp", Vp[:, :, :], T["Vb"][0:SEQ, :].rearrange("(j p) c -> p j c", p=128)[:, :, hp * 128:(hp + 1) * 128],
                  writes=[Vp.r])
            for I in range(16):
                q0 = 512 * I
                Qt = QTq.next()
                Sg = SGq.next()
                k.dma("sp", Qt[:, :], T["QT"][hp * 128:(hp + 1) * 128, q0:q0 + 512], writes=[Qt.r])
                k.dma("sp", Sg[:, :], T["SGT"][hp * 128:(hp + 1) * 128, q0:q0 + 512], writes=[Sg.r])
                k.op("dve", lambda h: h.memset(O[:, :], 0.0), writes=[O.r])
                jlast = 4 * I + 3
                steps = [(j, e) for j in range(jlast, -1, -1) for e in range(2)]
                stt = [dict() for _ in steps]
                nst = len(steps)

                def st_z(si):
                    j, e = steps[si]
                    Zt = Z.next()
                    stt[si]["Z"] = Zt
                    k.op("pe", lambda h, Zt=Zt, e=e, j=j, Qt=Qt: h.matmul(
                        out=Zt[:, :], lhsT=KTp[64 * e:64 * e + 64, 128 * j:128 * j + 128], rhs=Qt[64 * e:64 * e + 64, :],
                        start=True, stop=True), reads=[KTp.r, Qt.r], writes=[Zt.r])

                def st_e(si):
                    j, e = steps[si]
                    Zt = stt[si]["Z"]
                    hd = 2 * hp + e
                    Et = Er.next()
                    SPt = SPr.next()
                    stt[si]["E"], stt[si]["SP"] = Et, SPt
                    k.op("act", lambda h, Et=Et, Zt=Zt, hd=hd: h.activation(out=Et[:, :], in_=Zt[:, :], func=AF.Exp, bias=bias[:, hd:hd + 1],
                                                                            scale=1.0), reads=[Zt.r, bias.r], writes=[Et.r])
                    if j >= 4 * I:
                        m = j - 4 * I
                        k.op("pool", lambda h, Et=Et, m=m: h.tensor_tensor(out=Et[:, :], in0=Et[:, :], in1=maskd[:, m, :], op=ALU.mult),
                             reads=[Et.r, maskd.r], writes=[Et.r])
                    k.op("act", lambda h, Et=Et, SPt=SPt: h.activation(out=SPt[:, :], in_=Et[:, :], func=AF.Ln, bias=1.0, scale=1.0),
                         reads=[Et.r], writes=[SPt.r])

                def st_tri(si):
                    j, e = steps[si]
                    acc, SPt = ACC[e], stt[si]["SP"]
                    if j == jlast:
                        k.op("dve", lambda h, acc=acc: h.memset(acc[:, :], 0.0), writes=[acc.r])
                    k.op("pe", lambda h, acc=acc, SPt=SPt: h.matmul(out=acc[:, :], lhsT=tri[:, :], rhs=SPt[:, :], start=False, stop=True,
                                                                    skip_group_check=True), reads=[tri.r, SPt.r, acc.r], writes=[acc.r])

                def st_w(si):
                    j, e = steps[si]
                    acc = ACC[e]
                    Wt = Wr_.next()
                    stt[si]["W"] = Wt
                    k.op("act", lambda h, acc=acc, Wt=Wt: h.activation(out=Wt[:, :], in_=acc[:, :], func=AF.Exp, scale=-1.0),
                         reads=[acc.r], writes=[Wt.r])

                def st_comp(si):
                    j, e = steps[si]
                    acc, SPt, Et, Wt = ACC[e], stt[si]["SP"], stt[si]["E"], stt[si]["W"]
                    k.op("pe", lambda h, acc=acc, SPt=SPt: h.matmul(out=acc[:, :], lhsT=comp[:, :], rhs=SPt[:, :], start=False, stop=True,
                                                                    skip_group_check=True), reads=[comp.r, SPt.r, acc.r], writes=[acc.r])
                    At = Ar.next()
                    stt[si]["A"] = At
                    k.op("dve", lambda h, At=At, Et=Et, Wt=Wt: h.tensor_tensor(out=At[:, :], in0=Et[:, :], in1=Wt[:, :], op=ALU.mult),
                         reads=[Et.r, Wt.r], writes=[At.r])

                def st_av(si):
                    j, e = steps[si]
                    At = stt[si]["A"]
                    k.op("pe", lambda h, At=At, e=e, j=j: h.matmul(
                        out=O[64 * e:64 * e + 64, :], lhsT=Vp[:, j, 64 * e:64 * e + 64], rhs=At[:, :], start=False, stop=True,
                        skip_group_check=True), reads=[Vp.r, At.r, O.r], writes=[O.r])
                    stt[si].clear()

                stages = (st_z, st_e, st_tri, st_w, st_comp, st_av)
                for t in range(nst + len(stages) - 1):
                    for kk in range(len(stages) - 1, -1, -1):
                        si = t - kk
                        if 0 <= si < nst:
                            stages[kk](si)
                ot = oa.next()
                k.op("dve", lambda h, ot=ot, Sg=Sg: h.tensor_tensor(out=ot[:, :], in0=O[:, :], in1=Sg[:, :], op=ALU.mult),
                     reads=[O.r, Sg.r], writes=[ot.r])
                k.dma("sp", T["OAT"][hp * 128:(hp + 1) * 128, q0:q0 + 512], ot[:, :], reads=[ot.r])

        ident = sb(k, es, "identA", [128, 128], BF16)
        k.dma("sp", ident[:, :], T["c_ident"][:, :], writes=[ident.r])
        maskn = sb(k, es, "maskn", [128, 32], F32)
        k.dma("sp", maskn[:, :], T["c_maskn"][:, :], writes=[maskn.r])
        biasf = sb(k, es, "biasf", [128, 8, 4], F32)
        k.op("dve", lambda h: h.tensor_copy(out=biasf[:, :, :], in_=bias[:, :].unsqueeze(2).to_broadcast([128, 8, 4])),
             reads=[bias.r], writes=[biasf.r])
        pt_i = sb(k, es, "pt_i", [128, NS * NPAGE], I32)
        iota_f = sb(k, es, "iota_f", [128, 1], F32)
        idx = sb(k, es, "idx", [128, NS * NPAGE], I32)
        k.dma("sp", pt_i[:, :], T["page_table"].rearrange("b j -> (b j)").unsqueeze(0).to_broadcast([128, NS * NPAGE]),
              writes=[pt_i.r])
        k.dma("sp", iota_f[:, :], T["c_iota"][:, :], writes=[iota_f.r])
        idxf = sb(k, es, "idxf", [128, NS * NPAGE], F32)
        k.op("dve", lambda h: h.tensor_scalar(out=idxf[:, :], in0=pt_i[:, :], scalar1=128.0, scalar2=iota_f[:, 0:1], op0=ALU.mult,
                                              op1=ALU.add), reads=[pt_i.r, iota_f.r], writes=[idxf.r])
        k.op("dve", lambda h: h.tensor_scalar(out=idx[:, :], in0=idxf[:, :], scalar1=float(l * NPOOL * 128), scalar2=None,
                                              op0=ALU.add), reads=[idxf.r], writes=[idx.r])
        QS = sb(k, es, "QS", [128, 4, NST], BF16)
        KN = sb(k, es, "KN", [128, 4, NST], BF16)
        SGs = sb(k, es, "SGs", [128, 4, NST], F32)
        VN = sb(k, es, "VN", [128, 512], BF16)
        k.dma("sp", QS[:, :, :], T["QT"].rearrange("(hp p) t -> p hp t", p=128)[:, :, SEQ:SEQ + NST], writes=[QS.r])
        k.dma("sp", KN[:, :, :], T["KT"].rearrange("(hp p) t -> p hp t", p=128)[:, :, SEQ:SEQ + NST], writes=[KN.r])
        k.dma("sp", SGs[:, :, :], T["SGT"].rearrange("(hp p) t -> p hp t", p=128)[:, :, SEQ:SEQ + NST], writes=[SGs.r])
        Kf = sbring(k, es, "Kf", [128, 512], F32, 3)
        Vf = sbring(k, es, "Vf", [128, 512], F32, 3)
        Kb = sbring(k, es, "Kb", [128, 512], BF16, 2)
        Vb_ = sbring(k, es, "Vbs", [128, 512], BF16, 3)
        KTs = sbring(k, es, "KTs", [128, 4, 128], BF16, 2)
        zb = sbring(k, es, "zb", [128, 32], F32, 2)
        oas = sb(k, es, "oas", [128, 4, NST], BF16)
        psK = psring(k, es, "psK", [128, 8, 128], BF16, 1)
        ck = T["cache_k"].rearrange("l r c -> (l r) c")
        cv = T["cache_v"].rearrange("l r c -> (l r) c")
        k.op("dve", lambda h: h.memset(VN[:, :], 0.0), writes=[VN.r])
        Os = O
        Qb = sbring(k, es, "Qb", [128, 4, 16], BF16, 2)
        Knb = sbring(k, es, "Knb", [128, 4, 128], BF16, 2)
        for kb_ in Knb.bufs:
            k.op("pool", lambda h, kb_=kb_: h.memset(kb_[:, :, :], 0.0), writes=[kb_.r])
        for b in range(NS):
            k.dma("sp", VN[0:4, :], T["Vb"][SEQ + 4 * b:SEQ + 4 * b + 4, :], writes=[VN.r])
            qb_ = Qb.next()
            knb_ = Knb.next()
            k.op("pool", lambda h, qb_=qb_, b=b: h.tensor_copy(out=qb_[:, :, 0:4], in_=QS[:, :, 4 * b:4 * b + 4]), reads=[QS.r], writes=[qb_.r])
            k.op("pool", lambda h, knb_=knb_, b=b: h.tensor_copy(out=knb_[:, :, 0:4], in_=KN[:, :, 4 * b:4 * b + 4]), reads=[KN.r], writes=[knb_.r])
            k.op("dve", lambda h: h.memset(Os[:, 0:16], 0.0), writes=[Os.r])
            acc = ACC[b % 2]
            for blk in range(NPAGE, -1, -1):
                Zt = Z.next()
                if blk == NPAGE:
                    for hd in range(8):
                        hp, e = hd // 2, hd % 2
                        k.op("pe", lambda h, Zt=Zt, hp=hp, e=e, hd=hd, knb_=knb_, qb_=qb_: h.matmul(
                            out=Zt[:, 4 * hd:4 * hd + 4], lhsT=knb_[64 * e:64 * e + 64, hp, :],
                            rhs=qb_[64 * e:64 * e + 64, hp, 0:4], start=True, stop=True, skip_group_check=True),
                            reads=[knb_.r, qb_.r], writes=[Zt.r])
                    z_ = zb.next()
                    k.op("dve", lambda h, Zt=Zt, z_=z_: h.tensor_tensor(out=z_[:, :], in0=Zt[:, 0:32],
                                                                       in1=biasf[:, :, :].rearrange("p a b -> p (a b)"), op=ALU.add),
                         reads=[Zt.r, biasf.r], writes=[z_.r])
                    Et = Er.next()
                    k.op("act", lambda h, Et=Et, z_=z_: h.activation(out=Et[:, 0:32], in_=z_[:, :], func=AF.Exp),
                         reads=[z_.r], writes=[Et.r])
                    k.op("pool", lambda h, Et=Et: h.tensor_tensor(out=Et[:, 0:32], in0=Et[:, 0:32], in1=maskn[:, :], op=ALU.mult),
                         reads=[Et.r, maskn.r], writes=[Et.r])
                    Vl = VN
                else:
                    col = b * NPAGE + blk
                    kf = Kf.next()
                    vf = Vf.next()
                    k.raw_dma("pool", lambda h, kf=kf, col=col: h.indirect_dma_start(
                        out=kf[:, :], out_offset=None, in_=ck, in_offset=bass.IndirectOffsetOnAxis(ap=idx[:, col:col + 1], axis=0)),
                        reads=[idx.r], writes=[kf.r])
                    k.raw_dma("pool", lambda h, vf=vf, col=col: h.indirect_dma_start(
                        out=vf[:, :], out_offset=None, in_=cv, in_offset=bass.IndirectOffsetOnAxis(ap=idx[:, col:col + 1], axis=0)),
                        reads=[idx.r], writes=[vf.r])
                    kb = Kb.next()
                    vb = Vb_.next()
                    k.op("dve", lambda h, kb=kb, kf=kf: h.tensor_copy(out=kb[:, :], in_=kf[:, :]), reads=[kf.r], writes=[kb.r])
                    k.op("pool", lambda h, vb=vb, vf=vf: h.tensor_copy(out=vb[:, :], in_=vf[:, :]), reads=[vf.r], writes=[vb.r])
                    pk = psK.next()
                    for hp in range(4):
                        k.op("pe", lambda h, pk=pk, kb=kb, hp=hp: h.transpose(out=pk[:, hp, :], in_=kb[:, hp * 128:(hp + 1) * 128],
                                                                             identity=ident[:, :]),
                             reads=[kb.r, ident.r], writes=[pk.r])
                    kt = KTs.next()
                    k.op("act", lambda h, kt=kt, pk=pk: h.copy(out=kt[:, :, :], in_=pk[:, 0:4, :]), reads=[pk.r], writes=[kt.r])
                    for hd in range(8):
                        hp, e = hd // 2, hd % 2
                        k.op("pe", lambda h, Zt=Zt, kt=kt, hp=hp, e=e, hd=hd, qb_=qb_: h.matmul(
                            out=Zt[:, 4 * hd:4 * hd + 4], lhsT=kt[64 * e:64 * e + 64, hp, :],
                            rhs=qb_[64 * e:64 * e + 64, hp, 0:4], start=True, stop=True, skip_group_check=True),
                            reads=[kt.r, qb_.r], writes=[Zt.r])
                    z_ = zb.next()
                    k.op("dve", lambda h, Zt=Zt, z_=z_: h.tensor_tensor(out=z_[:, :], in0=Zt[:, 0:32],
                                                                       in1=biasf[:, :, :].rearrange("p a b -> p (a b)"), op=ALU.add),
                         reads=[Zt.r, biasf.r], writes=[z_.r])
                    Et = Er.next()
                    k.op("act", lambda h, Et=Et, z_=z_: h.activation(out=Et[:, 0:32], in_=z_[:, :], func=AF.Exp),
                         reads=[z_.r], writes=[Et.r])
                    Vl = vb
                SPt = SPr.next()
                k.op("act", lambda h, SPt=SPt, Et=Et: h.activation(out=SPt[:, 0:32], in_=Et[:, 0:32], func=AF.Ln, bias=1.0, scale=1.0),
                     reads=[Et.r], writes=[SPt.r])
                if blk == NPAGE:
                    k.op("dve", lambda h, acc=acc: h.memset(acc[:, 0:32], 0.0), writes=[acc.r])
                k.op("pe", lambda h, acc=acc, SPt=SPt: h.matmul(out=acc[:, 0:32], lhsT=tri[:, :], rhs=SPt[:, 0:32], start=False, stop=True,
                                                                skip_group_check=True), reads=[tri.r, SPt.r, acc.r], writes=[acc.r])
                Wt = Wr_.next()
                k.op("act", lambda h, acc=acc, Wt=Wt: h.activation(out=Wt[:, 0:32], in_=acc[:, 0:32], func=AF.Exp, scale=-1.0),
                     reads=[acc.r], writes=[Wt.r])
                k.op("pe", lambda h, acc=acc, SPt=SPt: h.matmul(out=acc[:, 0:32], lhsT=comp[:, :], rhs=SPt[:, 0:32], start=False, stop=True,
                                                                skip_group_check=True), reads=[comp.r, SPt.r, acc.r], writes=[acc.r])
                At = Ar.next()
                k.op("dve", lambda h, At=At, Et=Et, Wt=Wt: h.tensor_tensor(out=At[:, 0:32], in0=Et[:, 0:32], in1=Wt[:, 0:32], op=ALU.mult),
                     reads=[Et.r, Wt.r], writes=[At.r])
                for hd in range(8):
                    hp, e = hd // 2, hd % 2
                    k.op("pe", lambda h, At=At, Vl=Vl, hp=hp, e=e, hd=hd: h.matmul(
                        out=Os[64 * e:64 * e + 64, 4 * hp:4 * hp + 4], lhsT=Vl[:, 64 * hd:64 * hd + 64], rhs=At[:, 4 * hd:4 * hd + 4],
                        start=False, stop=True, skip_group_check=True), reads=[Vl.r, At.r, Os.r], writes=[Os.r])
            k.op("dve", lambda h, b=b: h.tensor_tensor(out=oas[:, :, 4 * b:4 * b + 4], in0=Os[:, 0:16].rearrange("p (a b) -> p a b", b=4),
                                                      in1=SGs[:, :, 4 * b:4 * b + 4], op=ALU.mult),
                 reads=[Os.r, SGs.r], writes=[oas.r])
        k.dma("sp", T["OAT"].rearrange("(hp p) t -> p hp t", p=128)[:, :, SEQ:SEQ + NST], oas[:, :, :], reads=[oas.r])
        k.barrier()
        k.emit()


def phase_B(k, l, T):
    with ExitStack() as es:
        pw = sb(k, es, "pw", [128, 4, 128], BF16)
        psc = sb(k, es, "psc", [128, 4], F32)
        invc = sb(k, es, "invc", [128, 4, 128], F32)
        k.dma("pool", pw[:, :, :], T["pool_w"][l].rearrange("g c d -> c g d"), writes=[pw.r])
        k.dma("sp", psc[:, :], T["pool_scale"][l].rearrange("(g p) -> p g", p=128), writes=[psc.r], allow_slow_non_contiguous=True)
        k.dma("sp", invc[:, :, :], T["c_invc"][:, :, :], writes=[invc.r])
        LP = 16 + 512
        Ur = sbring(k, es, "U", [128, LP], F32, 2)
        S1 = sbring(k, es, "S1", [128, LP], F32, 2)
        S2 = sbring(k, es, "S2", [128, LP], F32, 2)
        PGr = sbring(k, es, "PGb", [128, 512], F32, 2)
        pl = sbring(k, es, "plb", [128, 512], BF16, 2)
        ob = sbring(k, es, "obB", [128, 512], BF16, 2)
        pm = psring(k, es, "pmB", [128, 512], F32, 2)
        ident = sb(k, es, "identB", [128, 128], F32)
        k.dma("sp", ident[:, :], T["c_identf"][:, :], writes=[ident.r])

        def pool_block(g, U, G, L, first, pg_ap, pg_res, out_dst):
            Tn = L - 16
            n = G * Tn
            w = POOL_W[g]
            v3 = lambda buf: buf[:, 0:G * L].rearrange("p (g l) -> p g l", l=L)
            cur = U
            sh = 1
            lo = 0
            bufs = [S1.next(), S2.next()]
            bi = 0
            while sh < w:
                nxt = bufs[bi]
                bi ^= 1
                nlo = lo + sh
                k.op("pool", lambda h, cur=cur, nxt=nxt, nlo=nlo, sh=sh: h.tensor_tensor(
                    out=v3(nxt)[:, :, nlo:L], in0=v3(cur)[:, :, nlo:L], in1=v3(cur)[:, :, nlo - sh:L - sh], op=ALU.add),
                    reads=[cur.r], writes=[nxt.r])
                cur = nxt
                lo = nlo
                sh *= 2
            p_ = pl.next()
            pv = p_[:, 0:n].rearrange("p (g t) -> p g t", t=Tn)
            if first:
                k.op("dve", lambda h: h.tensor_tensor(out=v3(cur)[:, :, 16:L], in0=v3(cur)[:, :, 16:L],
                                                      in1=invc[:, g, 0:Tn].unsqueeze(1).to_broadcast([128, G, Tn]), op=ALU.mult),
                     reads=[cur.r, invc.r], writes=[cur.r])
                k.op("dve", lambda h: h.tensor_tensor(out=pv, in0=v3(cur)[:, :, 16:L], in1=v3(U)[:, :, 16:L], op=ALU.subtract),
                     reads=[cur.r, U.r], writes=[p_.r])
            else:
                k.op("dve", lambda h: h.scalar_tensor_tensor(out=pv, in0=v3(cur)[:, :, 16:L], scalar=1.0 / w, in1=v3(U)[:, :, 16:L],
                                                             op0=ALU.mult, op1=ALU.subtract), reads=[cur.r, U.r], writes=[p_.r])
            pm_ = pm.next()
            k.op("pe", lambda h: h.matmul(out=pm_[:, 0:n], lhsT=pw[:, g, :], rhs=p_[:, 0:n], start=True, stop=True),
                 reads=[pw.r, p_.r], writes=[pm_.r])
            o_ = ob.next()
            k.op("dve", lambda h: h.scalar_tensor_tensor(out=o_[:, 0:n], in0=pm_[:, 0:n], scalar=psc[:, g:g + 1], in1=pg_ap,
                                                         op0=ALU.mult, op1=ALU.mult), reads=[pm_.r, psc.r, pg_res], writes=[o_.r])
            k.dma("sp", out_dst, o_[:, 0:n], reads=[o_.r])

        for I in range(16):
            t0 = 512 * I
            for g in range(4):
                U = Ur.next()
                rows = slice(g * 128, (g + 1) * 128)
                if I == 0:
                    k.op("pool", lambda h, U=U: h.memset(U[:, 0:16], 0.0), writes=[U.r])
                    k.dma("sp", U[:, 16:LP], T["UT"][rows, 0:512], writes=[U.r])
                else:
                    k.dma("sp", U[:, :], T["UT"][rows, t0 - 16:t0 + 512], writes=[U.r])
                pg = PGr.next()
                k.dma("sp", pg[:, :], T["PGT"][rows, t0:t0 + 512], writes=[pg.r])
                if I == 0:
                    pool_block(g, U, 1, 16 + 128, True, pg[:, 0:128], pg.r, T["OBT"][rows, 0:128])
                    U2 = Ur.next()
                    k.dma("sp", U2[:, 0:16 + 384], T["UT"][rows, 128 - 16:512], writes=[U2.r])
                    pool_block(g, U2, 1, 16 + 384, False, pg[:, 128:512], pg.r, T["OBT"][rows, 128:512])
                else:
                    pool_block(g, U, 1, LP, False, pg[:, :], pg.r, T["OBT"][rows, t0:t0 + 512])

        spf = sbring(k, es, "spf", [120, 512], F32, 2)
        pst = psring(k, es, "pstB", [128, 512], F32, 2)
        for sgi in range(NSG):
            for g in range(4):
                rows = slice(g * 128, (g + 1) * 128)
                U = Ur.next()
                U3 = U[:, 0:16 * 20].rearrange("p (b l) -> p b l", l=20)
                k.op("pool", lambda h, U=U: h.memset(U[:, 0:16 * 20], 0.0), writes=[U.r])
                for half in range(2):
                    sp_ = spf.next()
                    s0_ = 16 * sgi + 8 * half
                    k.dma("sp", sp_[:, :], T["state_pool"][l, s0_:s0_ + 8, :, :].rearrange("b r c -> (b r) c"), writes=[sp_.r])
                    pt = pst.next()
                    k.op("pe", lambda h, pt=pt, sp_=sp_, g=g: h.transpose(out=pt[:, 0:120], in_=sp_[:, g * 128:(g + 1) * 128],
                                                                         identity=ident[0:120, 0:120]),
                         reads=[sp_.r, ident.r], writes=[pt.r])
                    k.op("act", lambda h, pt=pt, U3=U3, half=half: h.copy(out=U3[:, 8 * half:8 * half + 8, 1:16],
                                                                          in_=pt[:, 0:120].rearrange("p (b r) -> p b r", r=15)),
                         reads=[pt.r], writes=[U.r])
                c0_ = SEQ + 64 * sgi
                k.dma("sp", U3[:, :, 16:20], T["UT"][rows, c0_:c0_ + 64].rearrange("p (b t) -> p b t", t=4), writes=[U.r])
                pg = PGr.next()
                k.dma("sp", pg[:, 0:64], T["PGT"][rows, c0_:c0_ + 64], writes=[pg.r])
                pool_block(g, U, 16, 20, False, pg[:, 0:64], pg.r, T["OBT"][rows, c0_:c0_ + 64])
        k.barrier()
        k.emit()


def phase_C(k, l, T):
    with ExitStack() as es:
        ident = sb(k, es, "identC", [128, 128], BF16)
        k.dma("sp", ident[:, :], T["c_ident"][:, :], writes=[ident.r])
        cst = {}
        for tag, NT, NC in (("p", 128, 2), ("s", 64, 16)):
            SM = sb(k, es, "SM" + tag, [NT, NT], F32)
            CH = sb(k, es, "CH" + tag, [128, NC, NT], F32)
            RM = sb(k, es, "RM" + tag, [NT, NC], F32)
            RS = sb(k, es, "RS" + tag, [128, NT], F32)
            k.dma("sp", SM[:, :], T["c_sm_" + tag][:, :], writes=[SM.r])
            k.dma("sp", CH[:, :, :], T["c_ch_" + tag][:, :, :], writes=[CH.r])
            k.dma("sp", RM[:, :], T["c_rm_" + tag][:, :], writes=[RM.r])
            k.dma("sp", RS[:, :], T["c_rs_" + tag][:, :], writes=[RS.r])
            cst[tag] = (SM, CH, RM, RS)
        S = sb(k, es, "S", [128, 4, 256], F32)
        Sbf = sb(k, es, "Sbf", [128, 4, 256], BF16)
        k.op("dve", lambda h: h.memset(S[:, :, :], 0.0), writes=[S.r])
        k.op("pool", lambda h: h.memset(Sbf[:, :, :], 0.0), writes=[Sbf.r])
        gq = sbring(k, es, "gq", [128, 128], F32, 2)
        gk = sbring(k, es, "gk", [128, 128], F32, 2)
        la = sbring(k, es, "la", [128, 128], F32, 2)
        vv = sbring(k, es, "vv", [128, 256], BF16, 2)
        gg = sbring(k, es, "gg", [128, 256], F32, 2)
        bT = sbring(k, es, "bT", [128, 128], F32, 2)
        eb = sbring(k, es, "eb", [128, 128], F32, 2)
        enb = sbring(k, es, "enb", [128, 128], F32, 2)
        ebe = sbring(k, es, "ebe", [128, 128], F32, 2)
        ebend = sbring(k, es, "ebend", [128, 16], F32, 2)
        qb = sbring(k, es, "qb", [128, 128], BF16, 2)
        kb = sbring(k, es, "kb", [128, 128], BF16, 2)
        ke = sbring(k, es, "ke", [128, 128], BF16, 2)
        kendz = sbring(k, es, "kendz", [128, 2048], BF16, 2)
        qbz = sbring(k, es, "qbz", [128, 1024], BF16, 2)
        scm = sbring(k, es, "scm", [128, 128], BF16, 2)
        st = sbring(k, es, "stC", [128, 4], F32, 2)
        junk = sb(k, es, "junkC", [128, 256], BF16)
        oc = sbring(k, es, "oc", [128, 256], BF16, 2)
        ocT = sbring(k, es, "ocT", [128, 2, 128], BF16, 2)
        S0 = sbring(k, es, "S0", [128, 256], F32, 3)
        S0b = sbring(k, es, "S0b", [128, 256], BF16, 3)
        S1 = sbring(k, es, "S1o", [128, 256], F32, 3)
        psKE = ps(k, es, "psKE", [128, 1024], BF16)
        psSC = ps(k, es, "psSC", [128, 512], F32)
        psO = psring(k, es, "psO", [128, 512], F32, 2)
        psSU = psring(k, es, "psSU", [128, 512], F32, 2)
        psOT = ps(k, es, "psOT", [128, 8, 128], BF16)

        for (tt, NT) in tile_list():
            samp = tt >= SEQ
            tag = "s" if samp else "p"
            C = 4 if samp else 64
            NC = NT // C
            SM, CH, RM, RS = cst[tag]
            for hh in range(4):
                rows = slice(hh * 128, (hh + 1) * 128)
                q_, k_, a_, v_, g_ = gq.next(), gk.next(), la.next(), vv.next(), gg.next()
                k.dma("sp", q_[:, 0:NT], T["GQT"][rows, tt:tt + NT], writes=[q_.r])
                k.dma("sp", k_[:, 0:NT], T["GKT"][rows, tt:tt + NT], writes=[k_.r])
                k.dma("sp", a_[:, 0:NT], T["LAT"][rows, tt:tt + NT], writes=[a_.r])
                k.dma("sp", v_[0:NT, :], T["GV"][tt:tt + NT, hh * 256:(hh + 1) * 256], writes=[v_.r])
                k.dma("sp", g_[0:NT, :], T["GG"][tt:tt + NT, hh * 256:(hh + 1) * 256], writes=[g_.r])
                b_ = bT.next()
                k.op("dve", lambda h, b_=b_, a_=a_, NT=NT, RS=RS: h.tensor_tensor_scan(
                    out=b_[:, 0:NT], data0=RS[:, 0:NT], data1=a_[:, 0:NT], initial=0.0, op0=ALU.mult, op1=ALU.add),
                    reads=[a_.r, RS.r], writes=[b_.r])
                b3 = b_[:, 0:NT].rearrange("p (c t) -> p c t", t=C)
                e1, e2, e3, e4 = eb.next(), enb.next(), ebe.next(), ebend.next()
                k.op("act", lambda h, e1=e1, b_=b_, NT=NT: h.activation(out=e1[:, 0:NT], in_=b_[:, 0:NT], func=AF.Exp),
                     reads=[b_.r], writes=[e1.r])
                k.op("act", lambda h, e2=e2, b_=b_, NT=NT: h.activation(out=e2[:, 0:NT], in_=b_[:, 0:NT], func=AF.Exp, scale=-1.0),
                     reads=[b_.r], writes=[e2.r])
                k.op("pool", lambda h, e3=e3, b3=b3, NT=NT, C=C, NC=NC: h.tensor_tensor(
                    out=e3[:, 0:NT].rearrange("p (c t) -> p c t", t=C), in0=b3[:, :, C - 1:C].to_broadcast([128, NC, C]), in1=b3,
                    op=ALU.subtract), reads=[b_.r], writes=[e3.r])
                k.op("act", lambda h, e3=e3, NT=NT: h.activation(out=e3[:, 0:NT], in_=e3[:, 0:NT], func=AF.Exp), reads=[e3.r], writes=[e3.r])
                k.op("act", lambda h, e4=e4, b3=b3, NC=NC, C=C: h.activation(out=e4[:, 0:NC].unsqueeze(2), in_=b3[:, :, C - 1:C], func=AF.Exp),
                     reads=[b_.r], writes=[e4.r])
                qb_, kb_, ke_ = qb.next(), kb.next(), ke.next()
                k.op("dve", lambda h, qb_=qb_, q_=q_, e1=e1, NT=NT: h.tensor_tensor(out=qb_[:, 0:NT], in0=q_[:, 0:NT], in1=e1[:, 0:NT], op=ALU.mult),
                     reads=[q_.r, e1.r], writes=[qb_.r])
                k.op("pool", lambda h, kb_=kb_, k_=k_, e2=e2, NT=NT: h.tensor_tensor(out=kb_[:, 0:NT], in0=k_[:, 0:NT], in1=e2[:, 0:NT], op=ALU.mult),
                     reads=[k_.r, e2.r], writes=[kb_.r])
                k.op("pool", lambda h, ke_=ke_, k_=k_, e3=e3, NT=NT: h.tensor_tensor(out=ke_[:, 0:NT], in0=k_[:, 0:NT], in1=e3[:, 0:NT], op=ALU.mult),
                     reads=[k_.r, e3.r], writes=[ke_.r])
                k.op("pe", lambda h, ke_=ke_, NT=NT: h.transpose(out=psKE[0:NT, 0:128], in_=ke_[:, 0:NT], identity=ident[:, :]),
                     reads=[ke_.r, ident.r], writes=[psKE.r])
                kz = kendz.next()
                kz3 = kz[0:NT, 0:NC * 128].rearrange("p (c d) -> p c d", d=128)
                k.op("dve", lambda h, kz3=kz3, NT=NT, NC=NC, RM=RM: h.tensor_tensor(
                    out=kz3, in0=psKE[0:NT, 0:128].unsqueeze(1).to_broadcast([NT, NC, 128]),
                    in1=RM[0:NT, 0:NC].unsqueeze(2).to_broadcast([NT, NC, 128]), op=ALU.mult),
                    reads=[psKE.r, RM.r], writes=[kz.r])
                qz = qbz.next()
                qz3 = qz[:, 0:NC * NT].rearrange("p (c t) -> p c t", t=NT)
                k.op("pool", lambda h, qz3=qz3, qb_=qb_, NT=NT, NC=NC, CH=CH: h.tensor_tensor(
                    out=qz3, in0=qb_[:, 0:NT].unsqueeze(1).to_broadcast([128, NC, NT]), in1=CH[:, :, :], op=ALU.mult),
                    reads=[qb_.r, CH.r], writes=[qz.r])
                k.op("pe", lambda h, kb_=kb_, qb_=qb_, NT=NT: h.matmul(out=psSC[0:NT, 0:NT], lhsT=kb_[:, 0:NT], rhs=qb_[:, 0:NT],
                                                                      start=True, stop=True), reads=[kb_.r, qb_.r], writes=[psSC.r])
                sc_ = scm.next()
                k.op("dve", lambda h, sc_=sc_, NT=NT, SM=SM: h.tensor_tensor(out=sc_[0:NT, 0:NT], in0=psSC[0:NT, 0:NT], in1=SM[0:NT, 0:NT],
                                                                             op=ALU.mult), reads=[psSC.r, SM.r], writes=[sc_.r])
                O_ = psO.next()
                k.op("pe", lambda h, O_=O_, sc_=sc_, v_=v_, NT=NT: h.matmul(out=O_[0:NT, 0:256], lhsT=sc_[0:NT, 0:NT], rhs=v_[0:NT, :],
                                                                           start=True, stop=False), reads=[sc_.r, v_.r], writes=[O_.r])
                for c in range(NC):
                    if samp:
                        s0, s0b, s1 = S0.next(), S0b.next(), S1.next()
                        k.dma("sp", s0[:, :], T["state_gla"][l, 16 * ((tt - SEQ) // 64) + c, hh, :, :], writes=[s0.r])
                        k.op("pool", lambda h, s0=s0, s0b=s0b: h.tensor_copy(out=s0b[:, :], in_=s0[:, :]), reads=[s0.r], writes=[s0b.r])
                        Sb_ap, Sb_res = s0b[:, :], s0b.r
                    else:
                        Sb_ap, Sb_res = Sbf[:, hh, :], Sbf.r
                    k.op("pe", lambda h, O_=O_, qz3=qz3, c=c, Sb_ap=Sb_ap, NT=NT, NC=NC: h.matmul(
                        out=O_[0:NT, 0:256], lhsT=qz3[:, c, :], rhs=Sb_ap, start=False, stop=(c == NC - 1)),
                        reads=[qz.r, Sb_res], writes=[O_.r])
                    SU = psSU.next()
                    k.op("pe", lambda h, SU=SU, kz3=kz3, c=c, v_=v_, NT=NT: h.matmul(out=SU[:, 0:256], lhsT=kz3[:, c, :], rhs=v_[0:NT, :],
                                                                                     start=True, stop=True), reads=[kz.r, v_.r], writes=[SU.r])
                    if samp:
                        k.op("dve", lambda h, s1=s1, s0=s0, e4=e4, c=c, SU=SU: h.scalar_tensor_tensor(
                            out=s1[:, :], in0=s0[:, :], scalar=e4[:, c:c + 1], in1=SU[:, 0:256], op0=ALU.mult, op1=ALU.add),
                            reads=[s0.r, e4.r, SU.r], writes=[s1.r])
                        k.dma("sp", T["gla_s"][l, 16 * ((tt - SEQ) // 64) + c, hh, :, :], s1[:, :], reads=[s1.r])
                    else:
                        k.op("dve", lambda h, hh=hh, e4=e4, c=c, SU=SU: h.scalar_tensor_tensor(
                            out=S[:, hh, :], in0=S[:, hh, :], scalar=e4[:, c:c + 1], in1=SU[:, 0:256], op0=ALU.mult, op1=ALU.add),
                            reads=[S.r, e4.r, SU.r], writes=[S.r])
                        k.op("act", lambda h, hh=hh: h.copy(out=Sbf[:, hh, :], in_=S[:, hh, :]), reads=[S.r], writes=[Sbf.r])
                s_ = st.next()
                k.op("act", lambda h, O_=O_, s_=s_, NT=NT: h.activation(out=junk[0:NT, :], in_=O_[0:NT, 0:256], func=AF.Square,
                                                                        accum_out=s_[0:NT, 0:1]), reads=[O_.r], writes=[junk.r, s_.r])
                k.op("act", lambda h, s_=s_, NT=NT: h.activation(out=s_[0:NT, 1:2], in_=s_[0:NT, 0:1], func=AF.Ln, scale=1.0 / 256, bias=EPS),
                     reads=[s_.r], writes=[s_.r])
                k.op("act", lambda h, s_=s_, NT=NT: h.activation(out=s_[0:NT, 2:3], in_=s_[0:NT, 1:2], func=AF.Exp, scale=-0.5),
                     reads=[s_.r], writes=[s_.r])
                oc_ = oc.next()
                k.op("dve", lambda h, oc_=oc_, O_=O_, s_=s_, g_=g_, NT=NT: h.scalar_tensor_tensor(
                    out=oc_[0:NT, :], in0=O_[0:NT, 0:256], scalar=s_[0:NT, 2:3], in1=g_[0:NT, :], op0=ALU.mult, op1=ALU.mult),
                    reads=[O_.r, s_.r, g_.r], writes=[oc_.r])
                for ec in range(2):
                    k.op("pe", lambda h, oc_=oc_, ec=ec, NT=NT: h.transpose(out=psOT[:, ec, 0:NT], in_=oc_[0:NT, ec * 128:(ec + 1) * 128],
                                                                           identity=ident[0:NT, 0:NT]), reads=[oc_.r, ident.r], writes=[psOT.r])
                ot = ocT.next()
                k.op("act", lambda h, ot=ot, NT=NT: h.copy(out=ot[:, :, 0:NT], in_=psOT[:, 0:2, 0:NT]), reads=[psOT.r], writes=[ot.r])
                k.dma("sp", T["OCT"].rearrange("(hc p) t -> p hc t", p=128)[:, 2 * hh:2 * hh + 2, tt:tt + NT], ot[:, :, 0:NT], reads=[ot.r])
            if tt == SEQ - 128:
                k.dma("sp", T["gla_p"][l].rearrange("h d e -> d h e"), S[:, :, :], reads=[S.r])
        k.barrier()
        k.emit()


def phase_M(k, l, T):
    with ExitStack() as es:
        wpa = sb(k, es, "wpa", [128, 4, D], BF16)
        wpb = sb(k, es, "wpb", [128, 4, D], BF16)
        wpc = sb(k, es, "wpc", [128, 8, D], BF16)
        wo = sb(k, es, "wo", [128, 8, D], BF16)
        k.dma("pool", wpa[:, :, :], T["w_pa"][l].rearrange("(c p) n -> p c n", p=128), writes=[wpa.r])
        k.dma("pool", wpb[:, :, :], T["w_pb"][l].rearrange("(c p) n -> p c n", p=128), writes=[wpb.r])
        k.dma("pool", wpc[:, :, :], T["w_pc"][l].rearrange("(c p) n -> p c n", p=128), writes=[wpc.r])
        k.dma("pool", wo[:, :, :], T["w_o"][l].rearrange("(c p) n -> p c n", p=128), writes=[wo.r])
        OA = sbring(k, es, "OA", [128, 4, 512], BF16, 2)
        OB = sbring(k, es, "OB", [128, 4, 512], BF16, 2)
        OC = sbring(k, es, "OC", [128, 8, 512], BF16, 2)
        MG = sbring(k, es, "MG", [128, 512], F32, 6)
        acc = sbring(k, es, "accM", [128, 512], F32, 2)
        tmp = sbring(k, es, "tmpM", [128, 512], F32, 3)
        mT = sbring(k, es, "mT", [128, 8, 512], BF16, 2)
        xr = sbring(k, es, "xM", [128, D], F32, 2)
        yr = sbring(k, es, "yM", [128, D], F32, 2)
        pp = psring(k, es, "ppM", [128, 512], F32, 6)
        blocks = [(512 * i, 512) for i in range(16)] + [(SEQ, NST)]
        for (t0, n) in blocks:
            oa_, ob_, oc_ = OA.next(), OB.next(), OC.next()
            k.dma("sp", oa_[:, :, 0:n], T["OAT"].rearrange("(c p) t -> p c t", p=128)[:, :, t0:t0 + n], writes=[oa_.r])
            k.dma("sp", ob_[:, :, 0:n], T["OBT"].rearrange("(c p) t -> p c t", p=128)[:, :, t0:t0 + n], writes=[ob_.r])
            k.dma("sp", oc_[:, :, 0:n], T["OCT"].rearrange("(c p) t -> p c t", p=128)[:, :, t0:t0 + n], writes=[oc_.r])
            m_ = mT.next()
            for dmc in range(8):
                cs = slice(dmc * 128, (dmc + 1) * 128)
                mg = []
                for br in range(3):
                    g_ = MG.next()
                    r0 = br * D + dmc * 128
                    k.dma("sp", g_[:, 0:n], T["MGT"][r0:r0 + 128, t0:t0 + n], writes=[g_.r])
                    mg.append(g_)
                a_ = acc.next()
                for br, (src, w_, nfc) in enumerate(((oa_, wpa, 4), (ob_, wpb, 4), (oc_, wpc, 8))):
                    p_ = pp.next()
                    for fc in range(nfc):
                        k.op("pe", lambda h, p_=p_, w_=w_, fc=fc, cs=cs, src=src, n=n, nfc=nfc: h.matmul(
                            out=p_[:, 0:n], lhsT=w_[:, fc, cs], rhs=src[:, fc, 0:n], start=(fc == 0), stop=(fc == nfc - 1)),
                            reads=[w_.r, src.r], writes=[p_.r])
                    g_ = mg[br]
                    if br == 0:
                        k.op("dve", lambda h, a_=a_, p_=p_, g_=g_, n=n: h.tensor_tensor(out=a_[:, 0:n], in0=p_[:, 0:n], in1=g_[:, 0:n], op=ALU.mult),
                             reads=[p_.r, g_.r], writes=[a_.r])
                    else:
                        t_ = tmp.next()
                        k.op("dve", lambda h, t_=t_, p_=p_, g_=g_, n=n: h.tensor_tensor(out=t_[:, 0:n], in0=p_[:, 0:n], in1=g_[:, 0:n], op=ALU.mult),
                             reads=[p_.r, g_.r], writes=[t_.r])
                        if br == 1:
                            k.op("pool", lambda h, a_=a_, t_=t_, n=n: h.tensor_tensor(out=a_[:, 0:n], in0=a_[:, 0:n], in1=t_[:, 0:n], op=ALU.add),
                                 reads=[a_.r, t_.r], writes=[a_.r])
                        else:
                            k.op("pool", lambda h, a_=a_, t_=t_, m_=m_, dmc=dmc, n=n: h.tensor_tensor(
                                out=m_[:, dmc, 0:n], in0=a_[:, 0:n], in1=t_[:, 0:n], op=ALU.add), reads=[a_.r, t_.r], writes=[m_.r])
            tiles = [(t0 + i * 128, 128) for i in range(n // 128)] if n >= 128 else [(t0, n)]
            for (tt, np_) in tiles:
                c0 = tt - t0
                x_ = xr.next()
                if l == 0:
                    src = T["x_p"][tt:tt + np_, :] if tt < SEQ else T["x_s"][tt - SEQ:tt - SEQ + np_, :]
                else:
                    src = T["Y"][tt:tt + np_, :]
                k.dma("sp", x_[0:np_, :], src, writes=[x_.r])
                y_ = yr.next()
                for hf in range(2):
                    p_ = pp.next()
                    for dmc in range(8):
                        k.op("pe", lambda h, p_=p_, m_=m_, dmc=dmc, c0=c0, np_=np_, hf=hf: h.matmul(
                            out=p_[0:np_, :], lhsT=m_[:, dmc, c0:c0 + np_], rhs=wo[:, dmc, hf * 512:(hf + 1) * 512],
                            start=(dmc == 0), stop=(dmc == 7)), reads=[m_.r, wo.r], writes=[p_.r])
                    k.op("dve", lambda h, y_=y_, p_=p_, x_=x_, np_=np_, hf=hf: h.tensor_tensor(
                        out=y_[0:np_, hf * 512:(hf + 1) * 512], in0=p_[0:np_, :], in1=x_[0:np_, hf * 512:(hf + 1) * 512], op=ALU.add),
                        reads=[p_.r, x_.r], writes=[y_.r])
                if l == DEPTH - 1:
                    dst = T["y_p"][tt:tt + np_, :] if tt < SEQ else T["y_s"][tt - SEQ:tt - SEQ + np_, :]
                else:
                    dst = T["Y"][tt:tt + np_, :]
                k.dma("sp", dst, y_[0:np_, :], reads=[y_.r])
        k.barrier()
        k.emit()


def _consts():
    bf = ml_dtypes.bfloat16
    c = {}
    c["c_ident"] = np.eye(128, dtype=np.float32).astype(bf)
    c["c_identf"] = np.eye(128, dtype=np.float32)
    jj, kk = np.meshgrid(np.arange(128), np.arange(128), indexing="ij")
    c["c_tri"] = (jj >= kk).astype(np.float32).astype(bf)
    c["c_comp"] = (jj < kk).astype(np.float32).astype(bf)
    kq = np.arange(128)[:, None, None] + 128 * np.arange(4)[None, :, None]
    c["c_maskd"] = (kq < np.arange(512)[None, None, :]).astype(np.float32)
    mn = np.zeros((128, 32), np.float32)
    for kx in range(4):
        for col in range(32):
            mn[kx, col] = 1.0 if kx < (col % 4) else 0.0
    c["c_maskn"] = mn
    c["c_iota"] = np.arange(128, dtype=np.float32)[:, None].copy()
    invc = np.zeros((128, 4, 128), np.float32)
    for g, w in enumerate(POOL_W):
        invc[:, g, :] = 1.0 / np.minimum(w, np.arange(128) + 1)[None, :]
    c["c_invc"] = invc
    for tag, NT, C in (("p", 128, 64), ("s", 64, 4)):
        NC = NT // C
        s_, t_ = np.meshgrid(np.arange(NT), np.arange(NT), indexing="ij")
        c["c_sm_" + tag] = ((s_ // C == t_ // C) & (s_ <= t_)).astype(np.float32)
        ch = np.zeros((128, NC, NT), np.float32)
        rm = np.zeros((NT, NC), np.float32)
        for cc in range(NC):
            ch[:, cc, cc * C:(cc + 1) * C] = 1.0
            rm[cc * C:(cc + 1) * C, cc] = 1.0
        c["c_ch_" + tag] = ch
        c["c_rm_" + tag] = rm
        rs = np.ones((128, NT), np.float32)
        rs[:, ::C] = 0.0
        c["c_rs_" + tag] = rs
    return c


_IN_SHAPES = {
    "x_p": ([SEQ, D], F32), "x_s": ([NST, D], F32),
    "cache_k": ([DEPTH, NPOOL * 128, 512], F32), "cache_v": ([DEPTH, NPOOL * 128, 512], F32),
    "state_pool": ([DEPTH, NS, 15, 512], F32), "state_gla": ([DEPTH, NS, 4, 128, 256], F32),
    "page_table": ([NS, NPAGE], I32), "norm_g": ([DEPTH, D], F32), "w_in": ([DEPTH, D, N_IN], F32),
    "sb_qnorm_g": ([DEPTH, 64], F32), "sb_knorm_g": ([DEPTH, 64], F32), "sb_bias": ([DEPTH, 8], F32),
    "pool_w": ([DEPTH, 4, 128, 128], F32), "pool_scale": ([DEPTH, 512], F32), "gla_w2": ([DEPTH, 16, 512], F32),
    "gla_b2": ([DEPTH, 512], F32), "gla_onorm_g": ([DEPTH, 256], F32), "w_pa": ([DEPTH, 512, D], F32),
    "w_pb": ([DEPTH, 512, D], F32), "w_pc": ([DEPTH, D, D], F32), "w_o": ([DEPTH, D, D], F32),
}
_OUT_SHAPES = {
    "y_p": [SEQ, D], "y_s": [NST, D], "k_p": [DEPTH, SEQ, 512], "v_p": [DEPTH, SEQ, 512],
    "pool_p": [DEPTH, 15, 512], "gla_p": [DEPTH, 4, 128, 256], "k_s": [DEPTH, NST, 512], "v_s": [DEPTH, NST, 512],
    "pool_s": [DEPTH, NS, 15, 512], "gla_s": [DEPTH, NS, 4, 128, 256],
}
_SCRATCH = {
    "QT": ([512, NSLOT], BF16), "KT": ([512, NSLOT], BF16), "Vb": ([NSLOT, 512], BF16),
    "SGT": ([512, NSLOT], F32), "UT": ([512, NSLOT], F32), "PGT": ([512, NSLOT], F32),
    "GQT": ([512, NSLOT], F32), "GKT": ([512, NSLOT], F32), "LAT": ([512, NSLOT], F32),
    "GV": ([NSLOT, D], BF16), "GG": ([NSLOT, D], F32), "MGT": ([3 * D, NSLOT], F32),
    "OAT": ([512, NSLOT], BF16), "OBT": ([512, NSLOT], BF16), "OCT": ([D, NSLOT], BF16),
    "Y": ([NSLOT, D], F32),
}


def build_nc(phases=("P", "A", "B", "C", "M"), layers=(0, 1), plan=None):
    nc = bass.Bass("TRN2", target_bir_lowering=False)
    T = {}
    consts = _consts()
    for name, (shape, dt) in _IN_SHAPES.items():
        T[name] = nc.dram_tensor(name, shape, dt, kind="ExternalInput").ap()
    for name, arr in consts.items():
        dt = BF16 if arr.dtype == ml_dtypes.bfloat16 else F32
        T[name] = nc.dram_tensor(name, list(arr.shape), dt, kind="ExternalInput").ap()
    for name, shape in _OUT_SHAPES.items():
        T[name] = nc.dram_tensor(name, shape, F32, kind="ExternalOutput").ap()
    for name, (shape, dt) in _SCRATCH.items():
        T[name] = nc.dram_tensor(name, shape, dt, kind="Internal").ap()
    fns = {"P": phase_P, "A": phase_A, "B": phase_B, "C": phase_C, "M": phase_M}
    with ExitStack() as es:
        k = K(nc, es)
        if plan is None:
            plan = [(ph, l) for l in layers for ph in phases]
        for ph, l in plan:
            fns[ph](k, l, T)
    return nc, consts


def kernel(x_prompt, x_sample, cache_k, cache_v, state_pool, state_gla, page_table, norm_g, w_in,
           sb_qnorm_g, sb_knorm_g, sb_bias, pool_w, pool_scale, gla_w2, gla_b2, gla_onorm_g, w_pa, w_pb, w_pc, w_o):
    ncores = NCORES
    nc, consts = build_nc()
    f = lambda a: np.ascontiguousarray(np.asarray(a, dtype=np.float32))
    ck = f(cache_k).reshape(DEPTH, NPOOL * 128, 512)
    cv = f(cache_v).reshape(DEPTH, NPOOL * 128, 512)
    shared = {
        "cache_k": ck, "cache_v": cv, "norm_g": f(norm_g), "w_in": f(w_in), "sb_qnorm_g": f(sb_qnorm_g),
        "sb_knorm_g": f(sb_knorm_g), "sb_bias": f(sb_bias), "pool_w": f(pool_w), "pool_scale": f(pool_scale),
        "gla_w2": f(gla_w2), "gla_b2": f(gla_b2), "gla_onorm_g": f(gla_onorm_g), "w_pa": f(w_pa), "w_pb": f(w_pb),
        "w_pc": f(w_pc), "w_o": f(w_o),
    }
    shared.update(consts)
    xp = f(x_prompt)
    xs = f(x_sample)
    sp = f(state_pool)
    sg = f(state_gla)
    pt = np.ascontiguousarray(np.asarray(page_table, dtype=np.int32))
    in_maps = []
    for c in range(ncores):
        m = dict(shared)
        m["x_p"] = xp[c]
        m["x_s"] = np.ascontiguousarray(xs[NS * c:NS * (c + 1)].reshape(NST, D))
        m["state_pool"] = np.ascontiguousarray(sp[:, NS * c:NS * (c + 1)])
        m["state_gla"] = np.ascontiguousarray(sg[:, NS * c:NS * (c + 1)])
        m["page_table"] = np.ascontiguousarray(pt[NS * c:NS * (c + 1)])
        in_maps.append(m)
    res = run_bass_kernel_spmd(nc, in_maps, core_ids=list(range(ncores)))
    R = res.results
    B = 2
    y_prompt = np.stack([R[b]["y_p"] for b in range(B)]).astype(np.float32)
    y_sample = np.concatenate([R[c]["y_s"].reshape(NS, 4, D) for c in range(ncores)], axis=0).astype(np.float32)
    k_prompt = np.stack([R[b]["k_p"] for b in range(B)], axis=1).reshape(DEPTH, B, SEQ, 8, 64).astype(np.float32)
    v_prompt = np.stack([R[b]["v_p"] for b in range(B)], axis=1).reshape(DEPTH, B, SEQ, 8, 64).astype(np.float32)
    pool_prompt = np.stack([R[b]["pool_p"] for b in range(B)], axis=1).astype(np.float32)
    gla_prompt = np.stack([R[b]["gla_p"] for b in range(B)], axis=1).astype(np.float32)
    k_sample = np.concatenate([R[c]["k_s"].reshape(DEPTH, NS, 4, 8, 64) for c in range(ncores)], axis=1).astype(np.float32)
    v_sample = np.concatenate([R[c]["v_s"].reshape(DEPTH, NS, 4, 8, 64) for c in range(ncores)], axis=1).astype(np.float32)
    pool_sample = np.concatenate([R[c]["pool_s"] for c in range(ncores)], axis=1).astype(np.float32)
    gla_sample = np.concatenate([R[c]["gla_s"] for c in range(ncores)], axis=1).astype(np.float32)
    return (y_prompt, y_sample, k_prompt, v_prompt, pool_prompt, gla_prompt, k_sample, v_sample, pool_sample, gla_sample)
```
